# Optimizing a Trainium2 kernel written in Bass

```python
import jax, jax.numpy as jnp
from jax import lax
import numpy as np

D_MODEL = 1024
BATCH = 16
SEQ = 2048
DEPTH = 1

D_PLE = 256
RWKV_WIDTH = D_MODEL // 2
HEAD_SIZE = 64
N_HEADS = RWKV_WIDTH // HEAD_SIZE
POOL_WIDTH = D_MODEL - RWKV_WIDTH
POOL_WINDOWS = (2, 4, 8, 16)
N_POOL_GROUPS = len(POOL_WINDOWS)
POOL_GROUP = POOL_WIDTH // N_POOL_GROUPS
DECAY_LORA = 64
AAA_LORA = 64
GATE_LORA = 128
RWKV_SPLITS = (RWKV_WIDTH, 2 * RWKV_WIDTH, 3 * RWKV_WIDTH, 3 * RWKV_WIDTH + DECAY_LORA, 3 * RWKV_WIDTH + DECAY_LORA + AAA_LORA)
RWKV_IN = 3 * RWKV_WIDTH + DECAY_LORA + AAA_LORA + GATE_LORA
IN_WIDTH = RWKV_IN + POOL_WIDTH
D_FF = -(-8 * D_MODEL // (3 * 256)) * 256
ALPHA = (2.0 * DEPTH) ** 0.25
BETA = (8.0 * DEPTH) ** -0.25
LN_EPS = 1e-5
GN_EPS = 1e-5 * HEAD_SIZE

kernel_name = 'hybrid_rwkv7_pool_deepnorm'


def layer_norm(x, g, b, eps=LN_EPS):
    xf = x.astype(jnp.float32)
    mu = jnp.mean(xf, axis=-1, keepdims=True)
    var = jnp.mean(jnp.square(xf - mu), axis=-1, keepdims=True)
    return ((xf - mu) * lax.rsqrt(var + eps) * g + b).astype(x.dtype)


def token_shift(z):
    return jnp.pad(z, ((0, 0), (1, 0), (0, 0)))[:, :-1]


def wkv7(r, w, k, v, a, b):
    B_, T_, H_, N_ = r.shape

    def step(S, inp):
        r_t, w_t, k_t, v_t, a_t, b_t = inp
        Sa = jnp.einsum('bhvk,bhk->bhv', S, a_t)
        S = S * w_t[:, :, None, :] + Sa[..., None] * b_t[:, :, None, :] + v_t[..., None] * k_t[:, :, None, :]
        y = jnp.einsum('bhvk,bhk->bhv', S, r_t)
        return S, y

    xs = tuple(jnp.moveaxis(t, 1, 0) for t in (r, w, k, v, a, b))
    S0 = jnp.zeros((B_, H_, N_, N_), jnp.float32)
    _, ys = lax.scan(step, S0, xs)
    return jnp.moveaxis(ys, 0, 1)


def rwkv7_time_mix(z, mu, w0, wl_up, a0, al_up, gl_up, k_k, k_a, r_k, lnx_g, lnx_b):
    B_, T_, _ = z.shape
    f32 = jnp.float32
    zs = z + (token_shift(z) - z) * mu
    r, k, v, wd, ad, gd = jnp.split(zs, RWKV_SPLITS, axis=-1)
    w_log = -jax.nn.softplus(-(w0 + jnp.tanh(wd) @ wl_up)) - 0.5
    decay = jnp.exp(-jnp.exp(w_log.astype(f32)))
    a = jax.nn.sigmoid(a0 + ad @ al_up)
    g = jax.nn.sigmoid(gd) @ gl_up

    def heads(t):
        return t.astype(f32).reshape(B_, T_, N_HEADS, HEAD_SIZE)

    kk = heads(k * k_k)
    kk = kk / jnp.maximum(jnp.sqrt(jnp.sum(kk * kk, axis=-1, keepdims=True)), 1e-12)
    k2 = k * (1.0 + (a - 1.0) * k_a)
    rh, kh, vh, ah = heads(r), heads(k2), heads(v), heads(a)
    y = wkv7(rh, heads(decay), kh, vh, -kk, kk * ah)
    m = jnp.mean(y, axis=-1, keepdims=True)
    var = jnp.mean(jnp.square(y - m), axis=-1, keepdims=True)
    y = (y - m) * lax.rsqrt(var + GN_EPS) * lnx_g.reshape(N_HEADS, HEAD_SIZE) + lnx_b.reshape(N_HEADS, HEAD_SIZE)
    y = y + jnp.sum(rh * kh * r_k, axis=-1, keepdims=True) * vh
    y = y.reshape(B_, T_, RWKV_WIDTH) * g.astype(f32)
    return y.astype(z.dtype)


def multiscale_pool(u, pool_w, pool_scale):
    B_, T_, _ = u.shape
    f32 = jnp.float32
    uf = u.astype(f32).reshape(B_, T_, N_POOL_GROUPS, POOL_GROUP)
    c = jnp.cumsum(uf, axis=1)
    pos = jnp.arange(1, T_ + 1, dtype=f32)
    outs = []
    for j, W in enumerate(POOL_WINDOWS):
        cj = c[:, :, j]
        c_prev = jnp.pad(cj, ((0, 0), (W, 0), (0, 0)))[:, :T_]
        cnt = jnp.minimum(pos, float(W))
        outs.append((cj - c_prev) / cnt[None, :, None] - uf[:, :, j])
    d = jnp.stack(outs, axis=2)
    y = jnp.einsum('btgc,gcd->btgd', d, pool_w.astype(f32)).reshape(B_, T_, POOL_WIDTH)
    return (y * pool_scale.astype(f32)).astype(u.dtype)


def setup_inputs(seed: int = 0) -> dict:
    key = jax.random.key(seed)
    ks = jax.random.split(key, 32)
    f32 = jnp.float32
    L, D = DEPTH, D_MODEL

    def nrm(k, shape, scale):
        return jax.random.normal(k, shape, f32) * scale

    lin = jnp.linspace(0.0, 1.0, RWKV_WIDTH, dtype=f32)
    return {
        'x': nrm(ks[0], (BATCH, SEQ, D), 1.0),
        'p': nrm(ks[1], (L, BATCH, SEQ, D_PLE), 1.0),
        'ln_in_g': 1.0 + nrm(ks[2], (D,), 0.02),
        'ln_in_b': nrm(ks[3], (D,), 0.02),
        'w_in': nrm(ks[4], (L, D, IN_WIDTH), D ** -0.5),
        'mu_shift': jax.random.uniform(ks[5], (L, RWKV_IN), f32),
        'w0': (-6.0 + 5.0 * lin ** 0.85)[None, :] + nrm(ks[6], (L, RWKV_WIDTH), 0.1),
        'wl_up': nrm(ks[7], (L, DECAY_LORA, RWKV_WIDTH), 0.1 * DECAY_LORA ** -0.5),
        'a0': nrm(ks[8], (L, RWKV_WIDTH), 0.1),
        'al_up': nrm(ks[9], (L, AAA_LORA, RWKV_WIDTH), 0.1 * AAA_LORA ** -0.5),
        'gl_up': nrm(ks[10], (L, GATE_LORA, RWKV_WIDTH), GATE_LORA ** -0.5),
        'k_k': 0.85 + nrm(ks[11], (L, RWKV_WIDTH), 0.05),
        'k_a': 1.0 + nrm(ks[12], (L, RWKV_WIDTH), 0.05),
        'r_k': nrm(ks[13], (L, N_HEADS, HEAD_SIZE), 0.1),
        'lnx_g': 1.0 + nrm(ks[14], (L, RWKV_WIDTH), 0.02),
        'lnx_b': nrm(ks[15], (L, RWKV_WIDTH), 0.02),
        'pool_w': nrm(ks[16], (L, N_POOL_GROUPS, POOL_GROUP, POOL_GROUP), POOL_GROUP ** -0.5),
        'pool_scale': 1.0 + nrm(ks[17], (L, POOL_WIDTH), 0.1),
        'w_out': nrm(ks[18], (L, D, D), BETA * D ** -0.5),
        'ln1_g': 1.0 + nrm(ks[19], (L, D), 0.02),
        'ln1_b': nrm(ks[20], (L, D), 0.02),
        'ffn_gate': nrm(ks[21], (L, D, D_FF), D ** -0.5),
        'ffn_up': nrm(ks[22], (L, D, D_FF), D ** -0.5),
        'ffn_down': nrm(ks[23], (L, D_FF, D), BETA * D_FF ** -0.5),
        'pl_proj': nrm(ks[24], (L, D_PLE, D), BETA * D_PLE ** -0.5),
        'pl_gate': nrm(ks[25], (L, D, D), D ** -0.5),
        'ln2_g': 1.0 + nrm(ks[26], (L, D), 0.02),
        'ln2_b': nrm(ks[27], (L, D), 0.02),
    }


def reference(x, p, ln_in_g, ln_in_b, w_in, mu_shift, w0, wl_up, a0, al_up, gl_up, k_k, k_a, r_k,
              lnx_g, lnx_b, pool_w, pool_scale, w_out, ln1_g, ln1_b, ffn_gate, ffn_up, ffn_down,
              pl_proj, pl_gate, ln2_g, ln2_b):
    h = layer_norm(x, ln_in_g, ln_in_b)
    for i in range(DEPTH):
        z = h @ w_in[i]
        z_rwkv, u = z[..., :RWKV_IN], z[..., RWKV_IN:]
        y_a = rwkv7_time_mix(z_rwkv, mu_shift[i], w0[i], wl_up[i], a0[i], al_up[i], gl_up[i],
                             k_k[i], k_a[i], r_k[i], lnx_g[i], lnx_b[i])
        y_b = multiscale_pool(u, pool_w[i], pool_scale[i])
        mix = jnp.concatenate([y_a, y_b], axis=-1) @ w_out[i]
        h = layer_norm(ALPHA * h + mix, ln1_g[i], ln1_b[i])
        f = (jax.nn.silu(h @ ffn_gate[i]) * (h @ ffn_up[i])) @ ffn_down[i]
        e = jax.nn.sigmoid(h @ pl_gate[i]) * (p[i] @ pl_proj[i])
        h = layer_norm(ALPHA * h + f + e, ln2_g[i], ln2_b[i])
    return h
```

```python
import math
from contextlib import ExitStack

import numpy as np
import concourse.bass as bass
import concourse.mybir as mybir
from concourse.bass_utils import run_bass_kernel_spmd

F32 = mybir.dt.float32
BF16 = mybir.dt.bfloat16
AF = mybir.ActivationFunctionType
ALU = mybir.AluOpType

D = 1024
SEQ = 2048
NCORE = 8
BPC = 2
TOK = BPC * SEQ
TB = 128
NBLK = SEQ // TB
N = BPC * TB
CH = 64
NCHK = TB // CH
RW = 512
RWKV_IN = 1792
IN_W = 2304
DFF = 2816
NFC = DFF // 128
DPLE = 256
ALPHA = 2.0 ** 0.25
LN_EPS = 1e-5
GN_EPS = 1e-5 * 64
CDEC = math.exp(-0.5)
HALO = 16

PC_LAYOUT = [("mu", 14), ("w0", 4), ("a0", 4), ("k_k", 4), ("k_a", 4), ("r_k", 4), ("lnx_g", 4),
             ("lnx_b", 4), ("pool_scale", 4), ("ln_in_g", 8), ("ln_in_b", 8), ("ln1_g", 8),
             ("ln1_b", 8), ("ln2_g", 8), ("ln2_b", 8)]
PC_OFF = {}
_o = 0
for _n, _c in PC_LAYOUT:
    PC_OFF[_n] = _o
    _o += _c
PC_N = _o
CS_ID, CS_SU, CS_IU, CS_SL, CS_I64, CS_RC = 0, 128, 192, 256, 320, 384
CS_N = 384 + 64


class Buf:
    def __init__(self, name, t, init_dep=None):
        self.name = name
        self.t = t
        self.writers = {}
        self.readers = {}
        if init_dep is not None:
            self.writers["fence"] = init_dep
        self.dsem = None
        self.dcount = 0

    def __getitem__(self, idx):
        return self.t[idx]


class Op:
    __slots__ = ("eng", "fn", "deps", "odeps", "signal", "sem", "semval", "is_dma", "dbuf", "cost", "lat", "idx", "fin")


class Prog:
    ENGS = ("pe", "act", "dve", "pool", "sp")

    def __init__(self, nc, stack):
        self.nc = nc
        self.stack = stack
        self.ops = {e: [] for e in self.ENGS}
        self.allbufs = []
        self.fence_op = None
        self.nsem = 0
        self.nops = 0

    def op(self, eng, fn, reads=(), writes=(), dma_buf=None, cost=300.0, lat=0.0):
        o = Op()
        o.cost = cost
        o.lat = lat
        o.idx = self.nops
        self.nops += 1
        o.fin = None
        o.eng = eng
        o.fn = fn
        o.signal = False
        o.sem = None
        o.semval = None
        o.is_dma = dma_buf is not None
        o.dbuf = dma_buf
        if o.is_dma and dma_buf.dsem is None:
            dma_buf.dsem = self.stack.enter_context(self.nc.semaphore("d%d" % self.nsem))
            self.nsem += 1
        deps = []
        for b in reads:
            deps.extend(b.writers.values())
        for b in writes:
            deps.extend(b.writers.values())
            for lst in b.readers.values():
                deps.extend(lst)
        seen = set()
        ud = []
        for d in deps:
            if id(d) not in seen:
                seen.add(id(d))
                ud.append(d)
        o.deps = [d for d in ud
                  if not (eng == "pe" and d.eng == "pe" and not d.is_dma and not o.is_dma)]
        o.odeps = [d for d in ud
                   if (eng == "pe" and d.eng == "pe" and not d.is_dma and not o.is_dma)]
        key = ("dma:" + dma_buf.name) if o.is_dma else eng
        wset = set(id(b) for b in writes)
        for b in writes:
            b.readers = {}
            b.writers[key] = o
        for b in reads:
            if id(b) not in wset:
                b.readers.setdefault(key, []).append(o)
        self.ops[eng].append(o)
        return o

    def fence(self, fbuf):
        reads = [b for b in self.allbufs]
        o = self.op("pool", lambda e: e.memset(fbuf.t[:], 0.0), reads=reads, writes=[fbuf] + reads)
        self.fence_op = o
        return o

    def schedule(self, quantum=400.0, hop=150.0):
        import heapq
        allops = sorted([o for e in self.ENGS for o in self.ops[e]], key=lambda o: o.idx)
        succ = {}
        npred = {}
        for o in allops:
            n_ = 0
            for d in o.deps:
                succ.setdefault(id(d), []).append((o, hop))
                n_ += 1
            for d in o.odeps:
                succ.setdefault(id(d), []).append((o, -d.lat))
                n_ += 1
            npred[id(o)] = n_
        bl = {}
        for o in reversed(allops):
            b = 0.0
            for (s_, h_) in succ.get(id(o), ()):
                t = bl[id(s_)] + h_
                if t > b:
                    b = t
            bl[id(o)] = b + o.cost + o.lat
        rdy = {}
        fut = {e: [] for e in self.ENGS}
        avail = {e: [] for e in self.ENGS}
        free = {e: 0.0 for e in self.ENGS}
        new = {e: [] for e in self.ENGS}
        for o in allops:
            if npred[id(o)] == 0:
                rdy[id(o)] = 0.0
                heapq.heappush(fut[o.eng], (0.0, o.idx, o))
        total = len(allops)
        done = 0
        while done < total:
            best = None
            for e in self.ENGS:
                fe = free[e]
                f_, a_ = fut[e], avail[e]
                qfe = int(fe / quantum)
                while f_ and int(f_[0][0] / quantum) <= qfe:
                    r_, i_, o_ = heapq.heappop(f_)
                    heapq.heappush(a_, (-bl[id(o_)], i_, o_))
                if a_:
                    nb_, i_, o_ = a_[0]
                    st = max(fe, rdy[id(o_)])
                    key = (int(st / quantum), nb_, i_)
                    src = 0
                elif f_:
                    r_, i_, o_ = f_[0]
                    st = r_
                    key = (int(st / quantum), -bl[id(o_)], i_)
                    src = 1
                else:
                    continue
                if best is None or key < best[0]:
                    best = (key, e, src, st, o_)
            key, e, src, st, o = best
            if src == 0:
                heapq.heappop(avail[e])
            else:
                heapq.heappop(fut[e])
            o.fin = st + o.cost + o.lat
            free[e] = st + o.cost
            new[e].append(o)
            done += 1
            for (s_, h_) in succ.get(id(o), ()):
                t = o.fin + h_ if h_ > 0 else 0.0
                if t > rdy.get(id(s_), 0.0):
                    rdy[id(s_)] = t
                npred[id(s_)] -= 1
                if npred[id(s_)] == 0:
                    heapq.heappush(fut[s_.eng], (rdy.get(id(s_), 0.0), s_.idx, s_))
        self.ops = new
        self.est_ns = max(free.values())

    def emit(self, sems, block, do_schedule=True):
        if do_schedule:
            self.schedule()
        for e in self.ENGS:
            for o in self.ops[e]:
                for d in o.deps:
                    d.signal = True
        for e in self.ENGS:
            cnt = 0
            for o in self.ops[e]:
                if o.is_dma:
                    o.signal = True
                    b = o.dbuf
                    b.dcount += 1
                    o.sem = b.dsem
                    o.semval = 16 * b.dcount
                elif o.signal:
                    cnt += 1
                    o.sem = sems[e]
                    o.semval = cnt
        engobj = {"pe": "tensor", "act": "scalar", "dve": "vector", "pool": "gpsimd", "sp": "sync"}
        for e in self.ENGS:
            ops = self.ops[e]
            if not ops:
                continue

            def body(eng, ops=ops):
                waited = {}
                for o in ops:
                    need = {}
                    for d in o.deps:
                        k = d.sem
                        if d.semval > need.get(k, 0):
                            need[k] = d.semval
                    for k, v in need.items():
                        if waited.get(k, 0) >= v:
                            continue
                        eng.wait_ge(k, v)
                        waited[k] = v
                    ins = o.fn(eng)
                    if o.signal:
                        ins.then_inc(o.sem, 16 if o.is_dma else 1)
                last = {}
                for o in ops:
                    if o.is_dma:
                        last[o.sem] = o.semval
                for k, v in last.items():
                    if waited.get(k, 0) < v:
                        eng.wait_ge(k, v)

            getattr(block, engobj[e])(body)


_NC_CACHE = {}


def build_program(dbg=None, nblk1=NBLK, nblk2=NBLK):
    nc = bass.Bass("TRN2", target_bir_lowering=False)

    def din(name, shape):
        return nc.dram_tensor(name, list(shape), F32, kind="ExternalInput").ap()

    x_d = din("x", [TOK, D])
    p_d = din("p", [TOK, DPLE])
    w_in_d = din("w_in", [D, IN_W])
    w_out_d = din("w_out", [D, D])
    lora_d = din("lora_w", [128, RW])
    gl_d = din("gl_up", [128, RW])
    poolw_d = din("pool_w", [128, 4 * 128])
    wg_d = din("ffn_gate", [D, DFF])
    wu_d = din("ffn_up", [D, DFF])
    wd_d = din("ffn_down", [DFF, D])
    plp_d = din("pl_proj", [DPLE, D])
    plg_d = din("pl_gate", [D, D])
    pc_d = din("pcols", [128, PC_N])
    cs_d = din("consts", [128, CS_N])
    out_d = nc.dram_tensor("out", [TOK, D], F32, kind="ExternalOutput").ap()
    h1s_d = nc.dram_tensor("h1s", [8, 128, TOK], F32, kind="Internal").ap()
    dbg_d = {}
    if dbg:
        for name, width in dbg.items():
            dbg_d[name] = nc.dram_tensor("dbg_" + name, [128, width], F32, kind="ExternalOutput").ap()

    glob = ExitStack()
    with glob:
        P = Prog(nc, glob)
        phase = [1]
        sems = {e: glob.enter_context(nc.semaphore("s_" + e)) for e in Prog.ENGS}
        h1s = [Buf("h1s%d" % kc, h1s_d) for kc in range(8)]
        cnt = [0]

        def mk_alloc(stack, init_dep_fn):
            def sb(shape, dt, name=None):
                cnt[0] += 1
                nm = (name or "t") + "_%d" % cnt[0]
                b = Buf(nm, stack.enter_context(nc.sbuf_tensor(nm, list(shape), dt)), init_dep_fn())
                P.allbufs.append(b)
                return b

            def ps(name=None):
                cnt[0] += 1
                nm = (name or "ps") + "_%d" % cnt[0]
                b = Buf(nm, stack.enter_context(nc.psum_tensor(nm, [128, 512], F32)), init_dep_fn())
                P.allbufs.append(b)
                return b
            return sb, ps

        def fsz(ap):
            n = 1
            for d_ in ap.shape[1:]:
                n *= d_
            return n

        def ecost(eng, ap, psum=False):
            n = fsz(ap)
            if eng == "act":
                return 190.0 + n / 1.15
            if eng == "dve":
                return (130.0 if psum else 70.0) + n / 0.96
            return 250.0 + n / 0.33

        def mm(o, oap, l, lap, r, rap, start=True, stop=True, tp=None):
            c_ = max(64, fsz(rap)) * (4 if lap.dtype == F32 else 1) / 1.95 + 8.0
            if tp is not None:
                c_ *= 0.6
            if tp is None:
                P.op("pe", lambda e: e.matmul(oap, lap, rap, start=start, stop=stop), reads=[l, r], writes=[o],
                     cost=c_, lat=180.0)
            else:
                P.op("pe", lambda e: e.matmul(oap, lap, rap, start=start, stop=stop, tile_position=tp),
                     reads=[l, r], writes=[o], cost=c_, lat=180.0)

        def tr(o, oap, i, iap):
            P.op("pe", lambda e: e.transpose(oap, iap, identF), reads=[i, cs], writes=[o], cost=140.0, lat=180.0)

        def act(o, oap, i, iap, func, bias=None, scale=None, extra=()):
            kw = {}
            if bias is not None:
                kw["bias"] = bias
            if scale is not None:
                kw["scale"] = scale
            P.op("act", lambda e: e.activation(out=oap, in_=iap, func=func, **kw),
                 reads=[i] + list(extra), writes=[o], cost=ecost("act", oap))

        def tt(eng, o, oap, a, aap, b, bap, op):
            P.op(eng, lambda e: e.tensor_tensor(out=oap, in0=aap, in1=bap, op=op), reads=[a, b], writes=[o],
                 cost=ecost(eng, oap, True))

        def ts(eng, o, oap, a, aap, s1, s2, op0, op1=None, extra=()):
            if op1 is None:
                P.op(eng, lambda e: e.tensor_scalar(out=oap, in0=aap, scalar1=s1, scalar2=None, op0=op0),
                     reads=[a] + list(extra), writes=[o], cost=ecost(eng, oap))
            else:
                P.op(eng, lambda e: e.tensor_scalar(out=oap, in0=aap, scalar1=s1, scalar2=s2, op0=op0, op1=op1),
                     reads=[a] + list(extra), writes=[o], cost=ecost(eng, oap))

        def stt(eng, o, oap, a, aap, s, b, bap, op0, op1, extra=()):
            P.op(eng, lambda e: e.scalar_tensor_tensor(out=oap, in0=aap, scalar=s, in1=bap, op0=op0, op1=op1),
                 reads=[a, b] + list(extra), writes=[o], cost=ecost(eng, oap, True))

        def cp(eng, o, oap, i, iap):
            if eng == "act":
                P.op("act", lambda e: e.copy(out=oap, in_=iap), reads=[i], writes=[o], cost=ecost("act", oap))
            else:
                P.op(eng, lambda e: e.tensor_copy(out=oap, in_=iap), reads=[i], writes=[o], cost=ecost(eng, oap, True))

        def dma(eng, o, oap, i, iap, sbuf_side):
            rd = [i] if i is not None else []
            wr = [o] if o is not None else []
            nbytes = fsz(oap) * oap.shape[0] * 4
            P.op(eng, lambda e: e.dma_start(out=oap, in_=iap), reads=rd, writes=wr, dma_buf=sbuf_side,
                 cost=(80.0 if eng == "sp" else 1200.0), lat=2000.0 + nbytes / 150.0)

        def rsqrt(o, oap, i, iap, eps=None, floor=None):
            if phase[0] == 1:
                if floor is None:
                    ec = epsc[:iap.shape[0], {0: 0, 1: 1}[eps]:{0: 1, 1: 2}[eps]]
                    P.op("act", lambda e: e.activation(out=oap, in_=iap, func=AF.Ln, bias=ec, scale=1.0),
                         reads=[i, epsc], writes=[o], cost=ecost("act", oap))
                else:
                    P.op("dve", lambda e: e.tensor_scalar(out=oap, in0=iap, scalar1=floor * floor, scalar2=None, op0=ALU.max),
                         reads=[i], writes=[o], cost=ecost("dve", oap, True))
                    P.op("act", lambda e: e.activation(out=oap, in_=oap, func=AF.Ln), reads=[o], writes=[o],
                         cost=ecost("act", oap))
                P.op("act", lambda e: e.activation(out=oap, in_=oap, func=AF.Exp, scale=-0.5), reads=[o], writes=[o],
                     cost=ecost("act", oap))
                return
            if floor is None:
                P.op("act", lambda e: e.activation(out=oap, in_=iap, func=AF.Sqrt, bias=epsc[:iap.shape[0], {0: 0, 1: 1}[eps]:{0: 1, 1: 2}[eps]], scale=1.0),
                     reads=[i, epsc], writes=[o], cost=ecost("act", oap))
            else:
                P.op("act", lambda e: e.activation(out=oap, in_=iap, func=AF.Sqrt), reads=[i], writes=[o], cost=ecost("act", oap))
                P.op("dve", lambda e: e.tensor_scalar(out=oap, in0=oap, scalar1=floor, scalar2=None, op0=ALU.max),
                     reads=[o], writes=[o], cost=ecost("dve", oap))
            P.op("dve", lambda e: e.reciprocal(out=oap, in_=oap), reads=[o], writes=[o], cost=ecost("dve", oap))

        def memset(eng, o, oap, val):
            P.op(eng, lambda e: e.memset(oap, val), writes=[o], cost=ecost(eng, oap))

        def dump(name, b, ap=None):
            if dbg and name in dbg_d:
                dma("sp", None, dbg_d[name], b, b[:] if ap is None else ap, b)

        sbG, psG = mk_alloc(glob, lambda: None)
        pc = sbG([128, PC_N], F32, "pc")
        cs = sbG([128, CS_N], F32, "cs")
        identb = sbG([128, 128], BF16, "identb")
        id64b = sbG([128, 64], BF16, "id64b")
        bones = sbG([128, 128], BF16, "bones")
        bavg = sbG([128, 128], BF16, "bavg")
        onesD = sbG([128, 128], BF16, "onesD")
        onesDf = sbG([128, 128], F32, "onesDf")
        fenceb = sbG([128, 2], F32, "fence")
        omka = sbG([128, 4], F32, "omka")
        omu = sbG([128, 14], F32, "omu")
        epsc = sbG([128, 2], F32, "epsc")
        PSB = [psG("pb") for _ in range(8)]
        psi = [0]

        PSSPLIT = {"x": [0, 1, 2], "e": [3, 4, 5], "h": [6, 7], "all": list(range(8))}
        psc = {"x": 0, "e": 0, "h": 0, "all": 0}
        pskind = ["x"]

        def nps():
            k_ = pskind[0]
            g_ = PSSPLIT[k_]
            b = PSB[g_[psc[k_] % len(g_)]]
            psc[k_] += 1
            return b

        def pn(pb):
            return pb[:, 0:N]

        def v2(ap):
            return ap.rearrange("p (b t) -> p b t", b=2)

        def vc(ap):
            return ap.rearrange("p (b c t) -> p b c t", b=2, c=NCHK)

        dma("sp", pc, pc[:], None, pc_d, pc)
        dma("sp", cs, cs[:], None, cs_d, cs)
        cp("dve", identb, identb[:], cs, cs[:, CS_ID:CS_ID + 128])
        cp("dve", id64b, id64b[:], cs, cs[:, CS_I64:CS_I64 + 64])
        memset("pool", bones, bones[:], 0.0)
        memset("pool", bones, bones[0:64, 0:64], 1.0)
        memset("pool", bones, bones[64:128, 64:128], 1.0)
        memset("pool", bavg, bavg[:], 0.0)
        memset("pool", bavg, bavg[0:64, 0:64], 1.0 / 64)
        memset("pool", bavg, bavg[64:128, 64:128], 1.0 / 64)
        memset("pool", onesD, onesD[:], 1.0 / 1024)
        memset("pool", onesDf, onesDf[:], 1.0 / 1024)
        memset("pool", epsc, epsc[:, 0:1], LN_EPS)
        memset("pool", epsc, epsc[:, 1:2], GN_EPS)
        ts("dve", omka, omka[:], pc, pc[:, PC_OFF["k_a"]:PC_OFF["k_a"] + 4], -1.0, 1.0, ALU.mult, ALU.add)
        ts("dve", omu, omu[:], pc, pc[:, PC_OFF["mu"]:PC_OFF["mu"] + 14], -1.0, 1.0, ALU.mult, ALU.add)

        def pcol(name, j):
            return pc[:, PC_OFF[name] + j:PC_OFF[name] + j + 1]

        identF = cs[:, CS_ID:CS_ID + 128]
        mask_sl = cs[:, CS_SL:CS_SL + 64]
        masks2 = cs[:, CS_SU:CS_SU + 128]

        def layer_norm_fm(src, gname, bname, tmp_b, tmp_f):
            xb, sq = tmp_b
            mean, rstd, nmr = tmp_f
            pm = nps()
            pv = nps()
            for kc in range(8):
                act(sq[kc % 2], sq[kc % 2][:], src[kc], src[kc][:], AF.Square)
                if xb is None:
                    mm(pm, pn(pm), onesDf, onesDf[:], src[kc], src[kc][:], start=(kc == 0), stop=(kc == 7))
                else:
                    cp("act", xb[kc % 2], xb[kc % 2][:], src[kc], src[kc][:])
                    mm(pm, pn(pm), onesD, onesD[:], xb[kc % 2], xb[kc % 2][:], start=(kc == 0), stop=(kc == 7))
                mm(pv, pn(pv), onesD, onesD[:], sq[kc % 2], sq[kc % 2][:], start=(kc == 0), stop=(kc == 7))
            cp("act", mean, mean[:], pm, pn(pm))
            tt("dve", nmr, nmr[:], mean, mean[:], mean, mean[:], ALU.mult)
            tt("dve", rstd, rstd[:], pv, pn(pv), nmr, nmr[:], ALU.subtract)
            rsqrt(rstd, rstd[:], rstd, rstd[:], eps=0)
            stt("dve", nmr, nmr[:], mean, mean[:], -1.0, rstd, rstd[:], ALU.mult, ALU.mult)
            for kc in range(8):
                tt("dve", src[kc], src[kc][:], src[kc], src[kc][:], rstd, rstd[:], ALU.mult)
                tt("dve", src[kc], src[kc][:], src[kc], src[kc][:], nmr, nmr[:], ALU.add)
                act(src[kc], src[kc][:], src[kc], src[kc][:], AF.Identity,
                    bias=pcol(bname, kc), scale=pcol(gname, kc), extra=[pc])

        st1 = ExitStack()
        with st1:
            sb, _ = mk_alloc(st1, lambda: None)
            w_in = sb([128, 8, IN_W], BF16, "w_in")
            w_out = sb([128, 8, D], BF16, "w_out")
            loraw = sb([128, RW], BF16, "loraw")
            glw = sb([128, RW], BF16, "glw")
            poolw = sb([128, 512], BF16, "poolw")
            for kc in range(8):
                dma("pool", w_in, w_in[:, kc, :], None, w_in_d[kc * 128:(kc + 1) * 128, :], w_in)
            dma("pool", loraw, loraw[:], None, lora_d, loraw)
            dma("pool", glw, glw[:], None, gl_d, glw)
            dma("pool", poolw, poolw[:], None, poolw_d, poolw)
            for kc in range(8):
                dma("pool", w_out, w_out[:, kc, :], None, w_out_d[kc * 128:(kc + 1) * 128, :], w_out)

            xt = [sb([128, D], F32, "xt") for _ in range(2)]
            bst = [sb([128, 2, 6], F32, "bst") for _ in range(2)]
            mv = [sb([128, 4], F32, "mv") for _ in range(2)]
            h0f_a = [[sb([128, N], F32, "h0f") for _ in range(8)] for _ in range(2)]
            h0b_a = [[sb([128, N], BF16, "h0b") for _ in range(8)] for _ in range(2)]
            zst = [sb([128, 2, TB + 1], F32, "zst") for _ in range(3)]
            zc = sb([128, 14, 2], F32, "zc")
            zs = [sb([128, N], F32, "zs") for _ in range(12)]
            ztmp = [sb([128, N], F32, "ztmp") for _ in range(2)]
            lin = sb([128, N], BF16, "lin")
            gdin = sb([128, N], BF16, "gdin")
            u = [sb([128, 2, HALO + TB], F32, "u") for _ in range(4)]
            s2 = [sb([128, 2, HALO + TB], F32, "s2") for _ in range(2)]
            dsig = sb([128, 4, N], F32, "dsig")
            av = sb([128, 4, N], F32, "av")
            gv = sb([128, 4, N], F32, "gv")
            eL = sb([128, 4, N], F32, "eL")
            scm = sb([128, N], F32, "scm")
            tA = [sb([128, N], F32, "tA") for _ in range(8)]
            tB16 = [sb([128, N], BF16, "tB16") for _ in range(3)]
            AR = [sb([128, 2, NCHK, 2, CH], BF16, "AR") for _ in range(4)]
            BT = [sb([128, N], BF16, "BT") for _ in range(4)]
            KT = [sb([128, N], BF16, "KT") for _ in range(4)]
            VT = [sb([128, N], BF16, "VT") for _ in range(4)]
            bon = [sb([128, N], F32, "bon") for _ in range(4)]
            yT = sb([128, 4, 2, NCHK, CH], F32, "yT")
            ycat = [sb([128, N], BF16, "ycat") for _ in range(8)]
            Sf = sb([128, 8, CH], F32, "Sf")
            Sb = sb([128, 8, CH], BF16, "Sb")
            NA = [sb([128, 8, 2, CH], BF16, "NA") for _ in range(2)]
            KA = [sb([128, 8, 2, CH], BF16, "KA") for _ in range(2)]
            tokK = [sb([128, 8, CH], BF16, "tokK") for _ in range(2)]
            tokB = [sb([128, 8, CH], BF16, "tokB") for _ in range(2)]
            tokV = [sb([128, 8, CH], BF16, "tokV") for _ in range(2)]
            Pm = [sb([128, 8, CH], BF16, "Pm") for _ in range(3)]
            PTm = [sb([128, 8, CH], BF16, "PTm") for _ in range(3)]
            Xm = [sb([128, 8, CH], BF16, "Xm") for _ in range(3)]
            RHSb = sb([128, 8, CH], BF16, "RHSb")
            Ub = sb([128, 8, CH], BF16, "Ub")
            dSt = sb([128, 8, CH], F32, "dSt")
            pre = [sb([128, N], F32, "pre") for _ in range(8)]
            lnb = (None, [sb([128, N], BF16, "lnsq") for _ in range(2)])
            lnf = (sb([128, N], F32, "mean"), sb([128, N], F32, "rstd"), sb([128, N], F32, "nmr"))

            memset("pool", zc, zc[:], 0.0)
            for g in range(4):
                memset("pool", u[g], u[g][:], 0.0)
            memset("pool", Sf, Sf[:], 0.0)
            memset("pool", Sb, Sb[:], 0.0)
            memset("pool", scm, scm[:], 1.0)
            memset("pool", scm, scm[:].rearrange("p (c t) -> p c t", t=CH)[:, :, 0:1], 0.0)

            def tokslice(b, c):
                return slice(b * TB + c * CH, b * TB + (c + 1) * CH)

            for blk in range(nblk1):
                last = (blk == nblk1 - 1)
                h0f, h0b = h0f_a[blk % 2], h0b_a[blk % 2]
                pskind[0] = "x"
                xs = []
                for t2 in range(2):
                    i_ = t2
                    r0 = t2 * SEQ + blk * TB
                    X, S_, M_ = xt[i_], bst[i_], mv[i_]
                    xs.append(X)
                    dma("sp", X, X[:], None, x_d[r0:r0 + 128, :], X)
                    for c2 in range(2):
                        P.op("dve", lambda e, X=X, S_=S_, c2=c2: e.bn_stats(out=S_.t[:, c2, :], in_=X.t[:, c2 * 512:(c2 + 1) * 512]),
                             reads=[X], writes=[S_])
                    P.op("dve", lambda e, S_=S_, M_=M_: e.bn_aggr(out=M_.t[:, 0:2], in_=S_.t[:]),
                         reads=[S_], writes=[M_])
                    rsqrt(M_, M_[:, 2:3], M_, M_[:, 1:2], eps=0)
                    ts("dve", X, X[:], X, X[:], M_[:, 0:1], M_[:, 2:3], ALU.subtract, ALU.mult, extra=[M_])
                for kc in range(8):
                    pb = nps()
                    for t2 in range(2):
                        tr(pb, pb[:, t2 * 128:(t2 + 1) * 128], xs[t2], xs[t2][:, kc * 128:(kc + 1) * 128])
                    act(h0f[kc], h0f[kc][:], pb, pn(pb), AF.Identity, bias=pcol("ln_in_b", kc),
                        scale=pcol("ln_in_g", kc), extra=[pc])
                    act(h0b[kc], h0b[kc][:], pb, pn(pb), AF.Identity, bias=pcol("ln_in_b", kc),
                        scale=pcol("ln_in_g", kc), extra=[pc])
                if last:
                    dump("h0f0", h0f[0])

                for n in range(18):
                    pb = nps()
                    for kc in range(8):
                        mm(pb, pn(pb), w_in, w_in[:, kc, n * 128:(n + 1) * 128], h0b[kc], h0b[kc][:],
                           start=(kc == 0), stop=(kc == 7))
                    pbv = v2(pn(pb))
                    if n >= 14:
                        g = n - 14
                        cp("act", u[g], u[g][:, :, HALO:HALO + TB], pb, pbv)
                        continue
                    Z = zst[n % 3]
                    cp("act", Z, Z[:, :, 1:TB + 1], pb, pbv)
                    cp("dve", Z, Z[:, :, 0:1], zc, zc[:, n, :].unsqueeze(2))
                    cp("dve", zc, zc[:, n, :].unsqueeze(2), Z, Z[:, :, TB:TB + 1])
                    T_ = ztmp[n % 2]
                    Tv = v2(T_[:])
                    act(T_, Tv, Z, Z[:, :, 0:TB], AF.Copy, scale=pcol("mu", n), extra=[pc])
                    dst = zs[n] if n < 12 else T_
                    stt("dve", dst, v2(dst[:]), Z, Z[:, :, 1:TB + 1], omu[:, n:n + 1], T_, Tv, ALU.mult, ALU.add,
                        extra=[omu])
                    if n == 12:
                        act(lin, lin[0:64, :], T_, T_[0:64, :], AF.Tanh)
                        cp("act", lin, lin[64:128, :], T_, T_[64:128, :])
                    elif n == 13:
                        act(gdin, gdin[:], T_, T_[:], AF.Sigmoid)
                if last:
                    dump("zs0", zs[0])
                    dump("zs4", zs[4])

                for jc in range(4):
                    js = slice(jc * 128, (jc + 1) * 128)
                    pb = nps()
                    mm(pb, pn(pb), loraw, loraw[0:64, js], lin, lin[0:64, :])
                    act(dsig, dsig[:, jc, :], pb, pn(pb), AF.Sigmoid, bias=pcol("w0", jc), extra=[pc])
                    pb = nps()
                    mm(pb, pn(pb), loraw, loraw[64:128, js], lin, lin[64:128, :])
                    act(av, av[:, jc, :], pb, pn(pb), AF.Sigmoid, bias=pcol("a0", jc), extra=[pc])
                    pb = nps()
                    mm(pb, pn(pb), glw, glw[:, js], gdin, gdin[:])
                    cp("act", gv, gv[:, jc, :], pb, pn(pb))
                if last:
                    dump("dsig", dsig, dsig[:].rearrange("p j n -> p (j n)"))
                    dump("av", av, av[:].rearrange("p j n -> p (j n)"))
                    dump("gv", gv, gv[:].rearrange("p j n -> p (j n)"))

                for jc in range(4):
                    r_, k_, v_ = zs[jc], zs[4 + jc], zs[8 + jc]
                    kk, rinv, k2, bv, Ls, Lx, enL, eLx = tA
                    act(kk, kk[:], k_, k_[:], AF.Copy, scale=pcol("k_k", jc), extra=[pc])
                    act(tB16[0], tB16[0][:], kk, kk[:], AF.Square)
                    pb = nps()
                    mm(pb, pn(pb), bones, bones[:], tB16[0], tB16[0][:])
                    rsqrt(rinv, rinv[:], pb, pn(pb), floor=1e-12)
                    tt("dve", kk, kk[:], kk, kk[:], rinv, rinv[:], ALU.mult)
                    act(k2, k2[:], av, av[:, jc, :], AF.Identity, bias=omka[:, jc:jc + 1], scale=pcol("k_a", jc),
                        extra=[pc, omka])
                    tt("dve", k2, k2[:], k2, k2[:], k_, k_[:], ALU.mult)
                    tt("dve", bv, bv[:], kk, kk[:], av, av[:, jc, :], ALU.mult)
                    P.op("dve", lambda e, Ls=Ls, jc=jc: e.tensor_tensor_scan(
                        out=Ls.t[:], data0=scm.t[:], data1=dsig.t[:, jc, :], initial=0.0, op0=ALU.mult, op1=ALU.add),
                        reads=[scm, dsig], writes=[Ls])
                    tt("dve", Lx, Lx[:], Ls, Ls[:], dsig, dsig[:, jc, :], ALU.subtract)
                    act(eL, eL[:, jc, :], Ls, Ls[:], AF.Exp, scale=-CDEC)
                    act(enL, enL[:], Ls, Ls[:], AF.Exp, scale=CDEC)
                    act(eLx, eLx[:], Lx, Lx[:], AF.Exp, scale=-CDEC)
                    ARv = AR[jc]
                    tt("dve", ARv, ARv[:, :, :, 1, :], r_, vc(r_[:]), eL, vc(eL[:, jc, :]), ALU.mult)
                    stt("dve", ARv, ARv[:, :, :, 0, :], kk, vc(kk[:]), -1.0, eLx, vc(eLx[:]), ALU.mult, ALU.mult)
                    tt("dve", BT[jc], BT[jc][:], bv, bv[:], enL, enL[:], ALU.mult)
                    tt("dve", KT[jc], KT[jc][:], k2, k2[:], enL, enL[:], ALU.mult)
                    cp("act", VT[jc], VT[jc][:], v_, v_[:])
                    stt("dve", tB16[1], tB16[1][:], r_, r_[:], pcol("r_k", jc), k2, k2[:], ALU.mult, ALU.mult, extra=[pc])
                    pb = nps()
                    mm(pb, pn(pb), bones, bones[:], tB16[1], tB16[1][:])
                    tt("dve", bon[jc], bon[jc][:], pb, pn(pb), v_, v_[:], ALU.mult)
                    if last and jc == 0:
                        dump("kkn", kk)
                        dump("k2", k2)
                        dump("Ls", Ls)
                if last:
                    dump("eL", eL, eL[:].rearrange("p j n -> p (j n)"))
                    dump("bon0", bon[0])

                pskind[0] = "e"
                for c in range(NCHK):
                    par2 = c % 2
                    NAc, KAc, tK, tBk, tV = NA[par2], KA[par2], tokK[par2], tokB[par2], tokV[par2]
                    for (lhs, dstb) in ((BT, NAc), (KT, KAc)):
                        for grp in range(2):
                            pb = nps()
                            for q4 in range(4):
                                q = grp * 4 + q4
                                jc, b = q // 2, q % 2
                                for par in range(2):
                                    ps_ = slice(par * 64, par * 64 + 64)
                                    mm(pb, pb[ps_, q4 * 128:(q4 + 1) * 128], lhs[jc], lhs[jc][ps_, tokslice(b, c)],
                                       AR[jc], AR[jc][ps_, b, c, :, :].rearrange("p a t -> p (a t)"), tp=(par * 64, par * 64))
                            tt("dve", dstb, dstb[:, grp * 4:(grp + 1) * 4, :, :].rearrange("p q a t -> p q (a t)"),
                               pb, pb[:].rearrange("p (q x) -> p q x", q=4),
                               cs, masks2.unsqueeze(1).to_broadcast([128, 4, 128]), ALU.mult)
                    pb = nps()
                    for q in range(8):
                        jc, b = q // 2, q % 2
                        for par in range(2):
                            ps_ = slice(par * 64, par * 64 + 64)
                            mm(pb, pb[ps_, q * 64:(q + 1) * 64], AR[jc], AR[jc][ps_, b, c, 0, :],
                               BT[jc], BT[jc][ps_, tokslice(b, c)], tp=(par * 64, par * 64))
                    PT0 = PTm[0]
                    tt("dve", PT0, PT0[:], pb, pb[:].rearrange("p (q t) -> p q t", q=8),
                       cs, mask_sl.unsqueeze(1).to_broadcast([128, 8, 64]), ALU.mult)
                    for (src, dstb, eng) in ((KT, tK, "act"), (BT, tBk, "dve"), (VT, tV, "act")):
                        pb = nps()
                        for q in range(8):
                            jc, b = q // 2, q % 2
                            for par in range(2):
                                ps_ = slice(par * 64, par * 64 + 64)
                                mm(pb, pb[ps_, q * 64:(q + 1) * 64], src[jc], src[jc][ps_, tokslice(b, c)],
                                   identb, identb[ps_, ps_], tp=(par * 64, par * 64))
                        cp(eng, dstb, dstb[:], pb, pb[:].rearrange("p (q t) -> p q t", q=8))
                    X0 = Xm[0]
                    tt("dve", X0, X0[:], NAc, NAc[:, :, 0, :], id64b, id64b[:].unsqueeze(1).to_broadcast([128, 8, 64]), ALU.add)
                    Pc, PTc, Xc = None, PT0, X0
                    def Pap(ps_, q):
                        return (NAc, NAc[ps_, q, 0, :]) if Pc is None else (Pc, Pc[ps_, q, :])

                    for lvl in range(5):
                        PTn = PTm[(lvl + 1) % 3]
                        pb = nps()
                        for q in range(8):
                            for par in range(2):
                                ps_ = slice(par * 64, par * 64 + 64)
                                pb_, pa_ = Pap(ps_, q)
                                mm(pb, pb[ps_, q * 64:(q + 1) * 64], pb_, pa_, PTc, PTc[ps_, q, :],
                                   tp=(par * 64, par * 64))
                        cp("act", PTn, PTn[:], pb, pb[:].rearrange("p (q t) -> p q t", q=8))
                        if lvl < 4:
                            Pn = Pm[(lvl + 1) % 3]
                            pb = nps()
                            for q in range(8):
                                for par in range(2):
                                    ps_ = slice(par * 64, par * 64 + 64)
                                    pb_, pa_ = Pap(ps_, q)
                                    mm(pb, pb[ps_, q * 64:(q + 1) * 64], PTc, PTc[ps_, q, :], pb_, pa_,
                                       tp=(par * 64, par * 64))
                            cp("act", Pn, Pn[:], pb, pb[:].rearrange("p (q t) -> p q t", q=8))
                        else:
                            Pn = None
                        Xn = Xm[(lvl + 1) % 3]
                        pb = nps()
                        for q in range(8):
                            for par in range(2):
                                ps_ = slice(par * 64, par * 64 + 64)
                                mm(pb, pb[ps_, q * 64:(q + 1) * 64], PTn, PTn[ps_, q, :], Xc, Xc[ps_, q, :],
                                   tp=(par * 64, par * 64))
                        tt("dve", Xn, Xn[:], pb, pb[:].rearrange("p (q t) -> p q t", q=8), Xc, Xc[:], ALU.add)
                        Pc, PTc, Xc = Pn, PTn, Xn
                    pb = nps()
                    for q in range(8):
                        jc, b = q // 2, q % 2
                        for par in range(2):
                            ps_ = slice(par * 64, par * 64 + 64)
                            tp = (par * 64, par * 64)
                            mm(pb, pb[ps_, q * 64:(q + 1) * 64], AR[jc], AR[jc][ps_, b, c, 0, :], Sb, Sb[ps_, q, :],
                               start=True, stop=False, tp=tp)
                            mm(pb, pb[ps_, q * 64:(q + 1) * 64], KAc, KAc[ps_, q, 0, :], tV, tV[ps_, q, :],
                               start=False, stop=True, tp=tp)
                    cp("act", RHSb, RHSb[:], pb, pb[:].rearrange("p (q t) -> p q t", q=8))
                    pb = nps()
                    for q in range(8):
                        for par in range(2):
                            ps_ = slice(par * 64, par * 64 + 64)
                            mm(pb, pb[ps_, q * 64:(q + 1) * 64], Xc, Xc[ps_, q, :], RHSb, RHSb[ps_, q, :],
                               tp=(par * 64, par * 64))
                    cp("dve", Ub, Ub[:], pb, pb[:].rearrange("p (q t) -> p q t", q=8))
                    pby = nps()
                    for q in range(8):
                        jc, b = q // 2, q % 2
                        for par in range(2):
                            ps_ = slice(par * 64, par * 64 + 64)
                            tp = (par * 64, par * 64)
                            o_ = pby[ps_, q * 64:(q + 1) * 64]
                            mm(pby, o_, Sb, Sb[ps_, q, :], AR[jc], AR[jc][ps_, b, c, 1, :], start=True, stop=False, tp=tp)
                            mm(pby, o_, Ub, Ub[ps_, q, :], NAc, NAc[ps_, q, 1, :], start=False, stop=False, tp=tp)
                            mm(pby, o_, tV, tV[ps_, q, :], KAc, KAc[ps_, q, 1, :], start=False, stop=True, tp=tp)
                    pbs = nps()
                    for q in range(8):
                        for par in range(2):
                            ps_ = slice(par * 64, par * 64 + 64)
                            tp = (par * 64, par * 64)
                            o_ = pbs[ps_, q * 64:(q + 1) * 64]
                            mm(pbs, o_, tK, tK[ps_, q, :], tV, tV[ps_, q, :], start=True, stop=False, tp=tp)
                            mm(pbs, o_, tBk, tBk[ps_, q, :], Ub, Ub[ps_, q, :], start=False, stop=True, tp=tp)
                    tt("dve", dSt, dSt[:], pbs, pbs[:].rearrange("p (q t) -> p q t", q=8), Sf, Sf[:], ALU.add)
                    tt("dve", Sf, Sf[:].rearrange("p (j b) v -> p j b v", b=2),
                       dSt, dSt[:].rearrange("p (j b) v -> p j b v", b=2),
                       eL, eL[:].rearrange("p j (b c t) -> p j b c t", b=2, c=NCHK)[:, :, :, c, CH - 1:CH].to_broadcast([128, 4, 2, CH]),
                       ALU.mult)
                    cp("act", Sb, Sb[:], Sf, Sf[:])
                    cp("act", yT, yT[:, :, :, c, :], pby, pby[:].rearrange("p (j b t) -> p j b t", j=4, b=2))
                if last:
                    dump("yT", yT, yT[:].rearrange("p j b c t -> p (j b c t)"))
                    dump("Sf", Sf, Sf[:].rearrange("p q v -> p (q v)"))

                pskind[0] = "h"
                for jc in range(4):
                    y_ = yT[:, jc].rearrange("p b c t -> p (b c t)")
                    d_, yn = tA[0], tA[1]
                    cp("act", tB16[0], tB16[0][:], yT, y_)
                    pb = nps()
                    mm(pb, pn(pb), bavg, bavg[:], tB16[0], tB16[0][:])
                    tt("dve", d_, d_[:], yT, y_, pb, pn(pb), ALU.subtract)
                    act(tB16[1], tB16[1][:], d_, d_[:], AF.Square)
                    pb = nps()
                    mm(pb, pn(pb), bavg, bavg[:], tB16[1], tB16[1][:])
                    rsqrt(yn, yn[:], pb, pn(pb), eps=1)
                    tt("dve", yn, yn[:], yn, yn[:], d_, d_[:], ALU.mult)
                    act(yn, yn[:], yn, yn[:], AF.Identity, bias=pcol("lnx_b", jc), scale=pcol("lnx_g", jc), extra=[pc])
                    tt("dve", yn, yn[:], yn, yn[:], bon[jc], bon[jc][:], ALU.add)
                    tt("dve", ycat[jc], ycat[jc][:], yn, yn[:], gv, gv[:, jc, :], ALU.mult)

                for g in range(4):
                    W = 2 << g
                    U_ = u[g]
                    sa, sbb = s2[0], s2[1]
                    L_ = HALO + TB
                    tt("pool", sa, sa[:, :, 1:L_], U_, U_[:, :, 1:L_], U_, U_[:, :, 0:L_ - 1], ALU.add)
                    cur, oth = sa, sbb
                    sh = 2
                    lo = 1
                    while sh < W:
                        lo2 = lo + sh
                        tt("pool", oth, oth[:, :, lo2:L_], cur, cur[:, :, lo2:L_], cur, cur[:, :, lo2 - sh:L_ - sh], ALU.add)
                        cur, oth = oth, cur
                        lo = lo2
                        sh *= 2
                    dT = tB16[2]
                    dTv = v2(dT[:])
                    stt("dve", dT, dTv, cur, cur[:, :, HALO:L_], 1.0 / W, U_, U_[:, :, HALO:L_], ALU.mult, ALU.subtract)
                    if blk == 0:
                        rc = cs[:, CS_RC + g * 16:CS_RC + (g + 1) * 16].unsqueeze(1).to_broadcast([128, 2, 16])
                        tmpv = v2(tA[2][:])[:, :, 0:16]
                        tt("dve", tA[2], tmpv, cur, cur[:, :, HALO:HALO + 16], cs, rc, ALU.mult)
                        tt("dve", dT, dTv[:, :, 0:16], tA[2], tmpv, U_, U_[:, :, HALO:HALO + 16], ALU.subtract)
                    pb = nps()
                    mm(pb, pn(pb), poolw, poolw[:, g * 128:(g + 1) * 128], dT, dT[:])
                    act(ycat[4 + g], ycat[4 + g][:], pb, pn(pb), AF.Identity, scale=pcol("pool_scale", g), extra=[pc])
                    cp("pool", U_, U_[:, :, 0:HALO], U_, U_[:, :, TB:TB + HALO])

                for oc in range(8):
                    pb = nps()
                    for kc in range(8):
                        mm(pb, pn(pb), w_out, w_out[:, kc, oc * 128:(oc + 1) * 128], ycat[kc], ycat[kc][:],
                           start=(kc == 0), stop=(kc == 7))
                    stt("dve", pre[oc], pre[oc][:], h0f[oc], h0f[oc][:], ALPHA, pb, pn(pb), ALU.mult, ALU.add)
                if last:
                    dump("pre0", pre[0])
                layer_norm_fm(pre, "ln1_g", "ln1_b", lnb, lnf)
                if last:
                    dump("h1_0", pre[0])
                for kc in range(8):
                    dma("sp", h1s[kc], h1s_d[kc].rearrange("p (b s) -> p b s", b=2)[:, :, blk * TB:(blk + 1) * TB],
                        pre[kc], v2(pre[kc][:]), pre[kc])
            P.fence(fenceb)

        phase[0] = 2
        st2 = ExitStack()
        with st2:
            fo = P.fence_op
            sb, _ = mk_alloc(st2, lambda: fo)
            wg = sb([128, 8, DFF], BF16, "wg")
            wu = sb([128, 8, DFF], BF16, "wu")
            wd = sb([128, NFC, D], BF16, "wd")
            plg = sb([128, 8, D], BF16, "plg")
            plp = sb([128, 2, D], BF16, "plp")
            if nblk2 > 0:
                for kc in range(8):
                    dma("pool", wg, wg[:, kc, :], None, wg_d[kc * 128:(kc + 1) * 128, :], wg)
                    dma("pool", wu, wu[:, kc, :], None, wu_d[kc * 128:(kc + 1) * 128, :], wu)
                for kc in range(8):
                    dma("pool", plg, plg[:, kc, :], None, plg_d[kc * 128:(kc + 1) * 128, :], plg)
                for kc in range(2):
                    dma("pool", plp, plp[:, kc, :], None, plp_d[kc * 128:(kc + 1) * 128, :], plp)
                for fc in range(NFC):
                    dma("pool", wd, wd[:, fc, :], None, wd_d[fc * 128:(fc + 1) * 128, :], wd)
            h1f = [sb([128, N], F32, "h1f") for _ in range(8)]
            h1b = [sb([128, N], BF16, "h1b") for _ in range(8)]
            pt = [sb([128, DPLE], F32, "pt") for _ in range(4)]
            pTb = [sb([128, N], BF16, "pTb") for _ in range(2)]
            sg = [sb([128, N], F32, "sg") for _ in range(2)]
            actb = [sb([128, N], BF16, "actb") for _ in range(NFC)]
            et = [sb([128, N], F32, "et") for _ in range(2)]
            pre2 = [sb([128, N], F32, "pre2") for _ in range(8)]
            lnb2 = ([sb([128, N], BF16, "lnxb") for _ in range(2)], [sb([128, N], BF16, "lnsq") for _ in range(2)])
            lnf2 = (sb([128, N], F32, "mean"), sb([128, N], F32, "rstd"), sb([128, N], F32, "nmr"))
            ot = [sb([128, D], F32, "ot") for _ in range(1)]

            pskind[0] = "all"
            for blk in range(nblk2):
                for kc in range(8):
                    dma("sp", h1f[kc], v2(h1f[kc][:]), h1s[kc],
                        h1s_d[kc].rearrange("p (b s) -> p b s", b=2)[:, :, blk * TB:(blk + 1) * TB], h1f[kc])
                    cp("act", h1b[kc], h1b[kc][:], h1f[kc], h1f[kc][:])
                pts = []
                for t2 in range(2):
                    i_ = (blk % 2) * 2 + t2
                    r0 = t2 * SEQ + blk * TB
                    dma("sp", pt[i_], pt[i_][:], None, p_d[r0:r0 + 128, :], pt[i_])
                    pts.append(pt[i_])
                for k2_ in range(2):
                    pb = nps()
                    for t2 in range(2):
                        tr(pb, pb[:, t2 * 128:(t2 + 1) * 128], pts[t2], pts[t2][:, k2_ * 128:(k2_ + 1) * 128])
                    cp("act", pTb[k2_], pTb[k2_][:], pb, pn(pb))
                for fc in range(NFC):
                    fs = slice(fc * 128, (fc + 1) * 128)
                    pg = nps()
                    for kc in range(8):
                        mm(pg, pn(pg), wg, wg[:, kc, fs], h1b[kc], h1b[kc][:], start=(kc == 0), stop=(kc == 7))
                    pu = nps()
                    for kc in range(8):
                        mm(pu, pn(pu), wu, wu[:, kc, fs], h1b[kc], h1b[kc][:], start=(kc == 0), stop=(kc == 7))
                    S_ = sg[fc % 2]
                    act(S_, S_[:], pg, pn(pg), AF.Silu)
                    tt("dve", actb[fc], actb[fc][:], S_, S_[:], pu, pn(pu), ALU.mult)
                for oc in range(8):
                    os_ = slice(oc * 128, (oc + 1) * 128)
                    pe_ = nps()
                    for kc in range(8):
                        mm(pe_, pn(pe_), plg, plg[:, kc, os_], h1b[kc], h1b[kc][:], start=(kc == 0), stop=(kc == 7))
                    pp = nps()
                    for k2_ in range(2):
                        mm(pp, pn(pp), plp, plp[:, k2_, os_], pTb[k2_], pTb[k2_][:], start=(k2_ == 0), stop=(k2_ == 1))
                    E_ = et[oc % 2]
                    act(E_, E_[:], pe_, pn(pe_), AF.Sigmoid)
                    tt("dve", E_, E_[:], E_, E_[:], pp, pn(pp), ALU.mult)
                    pf = nps()
                    for fc in range(NFC):
                        mm(pf, pn(pf), wd, wd[:, fc, os_], actb[fc], actb[fc][:], start=(fc == 0), stop=(fc == NFC - 1))
                    stt("dve", pre2[oc], pre2[oc][:], h1f[oc], h1f[oc][:], ALPHA, pf, pn(pf), ALU.mult, ALU.add)
                    tt("dve", pre2[oc], pre2[oc][:], pre2[oc], pre2[oc][:], E_, E_[:], ALU.add)
                layer_norm_fm(pre2, "ln2_g", "ln2_b", lnb2, lnf2)
                for t2 in range(2):
                    r0 = t2 * SEQ + blk * TB
                    O_ = ot[0]
                    for hh in range(2):
                        pb = nps()
                        for kq in range(4):
                            kc = hh * 4 + kq
                            tr(pb, pb[:, kq * 128:(kq + 1) * 128], pre2[kc], pre2[kc][:, t2 * 128:(t2 + 1) * 128])
                        cp("act" if hh == 0 else "dve", O_, O_[:, hh * 512:(hh + 1) * 512], pb, pb[:])
                    dma("sp", None, out_d[r0:r0 + 128, :], O_, O_[:], O_)

            with nc.Block() as block:
                P.emit(sems, block)
            _NC_CACHE["est_ns"] = getattr(P, "est_ns", None)
    return nc


def _consts():
    cs = np.zeros((128, CS_N), np.float32)
    cs[:, CS_ID:CS_ID + 128] = np.eye(128, dtype=np.float32)
    pi = np.arange(128)[:, None] % 64
    ci = np.arange(64)[None, :]
    cs[:, CS_SU:CS_SU + 64] = (pi < ci)
    cs[:, CS_IU:CS_IU + 64] = (pi <= ci)
    cs[:, CS_SL:CS_SL + 64] = (pi > ci)
    cs[:, CS_I64:CS_I64 + 64] = (pi == ci)
    for g, W in enumerate((2, 4, 8, 16)):
        cnt = np.minimum(np.arange(1, 17), W).astype(np.float32)
        cs[:, CS_RC + g * 16:CS_RC + (g + 1) * 16] = (1.0 / cnt)[None, :]
    return cs


def _pcols(inp):
    pc = np.zeros((128, PC_N), np.float32)
    for name, ncol in PC_LAYOUT:
        v = np.asarray(inp["mu_shift" if name == "mu" else name], np.float32).reshape(-1)
        pc[:, PC_OFF[name]:PC_OFF[name] + ncol] = v.reshape(ncol, 128).T
    return pc


def _shared_inputs(inp):
    f = lambda a: np.ascontiguousarray(np.asarray(a, np.float32))
    sh = {
        "w_in": f(inp["w_in"][0]),
        "w_out": f(inp["w_out"][0]),
        "lora_w": f(np.concatenate([inp["wl_up"][0], inp["al_up"][0]], axis=0)),
        "gl_up": f(inp["gl_up"][0]),
        "pool_w": f(np.transpose(np.asarray(inp["pool_w"][0]), (1, 0, 2)).reshape(128, 512)),
        "ffn_gate": f(inp["ffn_gate"][0]),
        "ffn_up": f(inp["ffn_up"][0]),
        "ffn_down": f(inp["ffn_down"][0]),
        "pl_proj": f(inp["pl_proj"][0]),
        "pl_gate": f(inp["pl_gate"][0]),
        "pcols": _pcols(inp),
        "consts": _consts(),
    }
    return sh


def kernel(**inp):
    x = np.asarray(inp["x"], np.float32)
    p = np.asarray(inp["p"], np.float32)[0]
    if "nc" not in _NC_CACHE:
        _NC_CACHE["nc"] = build_program()
    nc = _NC_CACHE["nc"]
    sh = _shared_inputs(inp)
    in_maps = []
    for c in range(NCORE):
        m = dict(sh)
        m["x"] = np.ascontiguousarray(x[c * BPC:(c + 1) * BPC].reshape(TOK, D))
        m["p"] = np.ascontiguousarray(p[c * BPC:(c + 1) * BPC].reshape(TOK, DPLE))
        in_maps.append(m)
    res = run_bass_kernel_spmd(nc, in_maps, core_ids=list(range(NCORE)))
    out = np.stack([res.results[c]["out"].reshape(BPC, SEQ, D) for c in range(NCORE)], axis=0)
    return out.reshape(NCORE * BPC, SEQ, D).astype(np.float32)
```

```python
import math
from contextlib import ExitStack

import numpy as np
import concourse.bass as bass
import concourse.mybir as mybir
from concourse.bass_utils import run_bass_kernel_spmd

F32 = mybir.dt.float32
BF16 = mybir.dt.bfloat16
AF = mybir.ActivationFunctionType
ALU = mybir.AluOpType

D = 1024
SEQ = 2048
NCORE = 8
BPC = 2
TOK = BPC * SEQ
TB = 128
NBLK = SEQ // TB
N = BPC * TB
CH = 64
NCHK = TB // CH
RW = 512
RWKV_IN = 1792
IN_W = 2304
DFF = 2816
NFC = DFF // 128
DPLE = 256
ALPHA = 2.0 ** 0.25
LN_EPS = 1e-5
GN_EPS = 1e-5 * 64
CDEC = math.exp(-0.5)
HALO = 16

PC_LAYOUT = [("mu", 14), ("w0", 4), ("a0", 4), ("k_k", 4), ("k_a", 4), ("r_k", 4), ("lnx_g", 4),
             ("lnx_b", 4), ("pool_scale", 4), ("ln_in_g", 8), ("ln_in_b", 8), ("ln1_g", 8),
             ("ln1_b", 8), ("ln2_g", 8), ("ln2_b", 8)]
PC_OFF = {}
_o = 0
for _n, _c in PC_LAYOUT:
    PC_OFF[_n] = _o
    _o += _c
PC_N = _o
CS_ID, CS_SU, CS_IU, CS_SL, CS_I64, CS_RC = 0, 128, 192, 256, 320, 384
CS_N = 384 + 64


class Buf:
    def __init__(self, name, t, init_dep=None):
        self.name = name
        self.t = t
        self.writers = {}
        self.readers = {}
        if init_dep is not None:
            self.writers["fence"] = init_dep
        self.dsem = None
        self.dcount = 0

    def __getitem__(self, idx):
        return self.t[idx]


class Op:
    __slots__ = ("eng", "fn", "deps", "odeps", "signal", "sem", "semval", "is_dma", "dbuf", "cost", "lat", "idx", "fin")


class Prog:
    ENGS = ("pe", "act", "dve", "pool", "sp")

    def __init__(self, nc, stack):
        self.nc = nc
        self.stack = stack
        self.ops = {e: [] for e in self.ENGS}
        self.allbufs = []
        self.fence_op = None
        self.nsem = 0
        self.nops = 0

    def op(self, eng, fn, reads=(), writes=(), dma_buf=None, cost=300.0, lat=0.0):
        o = Op()
        o.cost = cost
        o.lat = lat
        o.idx = self.nops
        self.nops += 1
        o.fin = None
        o.eng = eng
        o.fn = fn
        o.signal = False
        o.sem = None
        o.semval = None
        o.is_dma = dma_buf is not None
        o.dbuf = dma_buf
        if o.is_dma and dma_buf.dsem is None:
            dma_buf.dsem = self.stack.enter_context(self.nc.semaphore("d%d" % self.nsem))
            self.nsem += 1
        deps = []
        for b in reads:
            deps.extend(b.writers.values())
        for b in writes:
            deps.extend(b.writers.values())
            for lst in b.readers.values():
                deps.extend(lst)
        seen = set()
        ud = []
        for d in deps:
            if id(d) not in seen:
                seen.add(id(d))
                ud.append(d)
        o.deps = [d for d in ud
                  if not (eng == "pe" and d.eng == "pe" and not d.is_dma and not o.is_dma)]
        o.odeps = [d for d in ud
                   if (eng == "pe" and d.eng == "pe" and not d.is_dma and not o.is_dma)]
        key = ("dma:" + dma_buf.name) if o.is_dma else eng
        wset = set(id(b) for b in writes)
        for b in writes:
            b.readers = {}
            b.writers[key] = o
        for b in reads:
            if id(b) not in wset:
                b.readers.setdefault(key, []).append(o)
        self.ops[eng].append(o)
        return o

    def fence(self, fbuf):
        reads = [b for b in self.allbufs]
        o = self.op("pool", lambda e: e.memset(fbuf.t[:], 0.0), reads=reads, writes=[fbuf] + reads)
        self.fence_op = o
        return o

    def schedule(self, quantum=400.0, hop=150.0):
        import heapq
        allops = sorted([o for e in self.ENGS for o in self.ops[e]], key=lambda o: o.idx)
        succ = {}
        npred = {}
        for o in allops:
            n_ = 0
            for d in o.deps:
                succ.setdefault(id(d), []).append((o, hop))
                n_ += 1
            for d in o.odeps:
                succ.setdefault(id(d), []).append((o, -d.lat))
                n_ += 1
            npred[id(o)] = n_
        bl = {}
        for o in reversed(allops):
            b = 0.0
            for (s_, h_) in succ.get(id(o), ()):
                t = bl[id(s_)] + h_
                if t > b:
                    b = t
            bl[id(o)] = b + o.cost + o.lat
        rdy = {}
        fut = {e: [] for e in self.ENGS}
        avail = {e: [] for e in self.ENGS}
        free = {e: 0.0 for e in self.ENGS}
        new = {e: [] for e in self.ENGS}
        for o in allops:
            if npred[id(o)] == 0:
                rdy[id(o)] = 0.0
                heapq.heappush(fut[o.eng], (0.0, o.idx, o))
        total = len(allops)
        done = 0
        while done < total:
            best = None
            for e in self.ENGS:
                fe = free[e]
                f_, a_ = fut[e], avail[e]
                qfe = int(fe / quantum)
                while f_ and int(f_[0][0] / quantum) <= qfe:
                    r_, i_, o_ = heapq.heappop(f_)
                    heapq.heappush(a_, (-bl[id(o_)], i_, o_))
                if a_:
                    nb_, i_, o_ = a_[0]
                    st = max(fe, rdy[id(o_)])
                    key = (int(st / quantum), nb_, i_)
                    src = 0
                elif f_:
                    r_, i_, o_ = f_[0]
                    st = r_
                    key = (int(st / quantum), -bl[id(o_)], i_)
                    src = 1
                else:
                    continue
                if best is None or key < best[0]:
                    best = (key, e, src, st, o_)
            key, e, src, st, o = best
            if src == 0:
                heapq.heappop(avail[e])
            else:
                heapq.heappop(fut[e])
            o.fin = st + o.cost + o.lat
            free[e] = st + o.cost
            new[e].append(o)
            done += 1
            for (s_, h_) in succ.get(id(o), ()):
                t = o.fin + h_ if h_ > 0 else 0.0
                if t > rdy.get(id(s_), 0.0):
                    rdy[id(s_)] = t
                npred[id(s_)] -= 1
                if npred[id(s_)] == 0:
                    heapq.heappush(fut[s_.eng], (rdy.get(id(s_), 0.0), s_.idx, s_))
        self.ops = new
        self.est_ns = max(free.values())

    def emit(self, sems, block, do_schedule=True):
        if do_schedule:
            self.schedule()
        for e in self.ENGS:
            for o in self.ops[e]:
                for d in o.deps:
                    d.signal = True
        for e in self.ENGS:
            cnt = 0
            for o in self.ops[e]:
                if o.is_dma:
                    o.signal = True
                    b = o.dbuf
                    b.dcount += 1
                    o.sem = b.dsem
                    o.semval = 16 * b.dcount
                elif o.signal:
                    cnt += 1
                    o.sem = sems[e]
                    o.semval = cnt
        engobj = {"pe": "tensor", "act": "scalar", "dve": "vector", "pool": "gpsimd", "sp": "sync"}
        for e in self.ENGS:
            ops = self.ops[e]
            if not ops:
                continue

            def body(eng, ops=ops):
                waited = {}
                for o in ops:
                    need = {}
                    for d in o.deps:
                        k = d.sem
                        if d.semval > need.get(k, 0):
                            need[k] = d.semval
                    for k, v in need.items():
                        if waited.get(k, 0) >= v:
                            continue
                        eng.wait_ge(k, v)
                        waited[k] = v
                    ins = o.fn(eng)
                    if o.signal:
                        ins.then_inc(o.sem, 16 if o.is_dma else 1)
                last = {}
                for o in ops:
                    if o.is_dma:
                        last[o.sem] = o.semval
                for k, v in last.items():
                    if waited.get(k, 0) < v:
                        eng.wait_ge(k, v)

            getattr(block, engobj[e])(body)


_NC_CACHE = {}


def build_program(dbg=None, nblk1=NBLK, nblk2=NBLK):
    nc = bass.Bass("TRN2", target_bir_lowering=False)

    def din(name, shape):
        return nc.dram_tensor(name, list(shape), F32, kind="ExternalInput").ap()

    x_d = din("x", [TOK, D])
    p_d = din("p", [TOK, DPLE])
    w_in_d = din("w_in", [D, IN_W])
    w_out_d = din("w_out", [D, D])
    lora_d = din("lora_w", [128, RW])
    gl_d = din("gl_up", [128, RW])
    poolw_d = din("pool_w", [128, 4 * 128])
    wg_d = din("ffn_gate", [D, DFF])
    wu_d = din("ffn_up", [D, DFF])
    wd_d = din("ffn_down", [DFF, D])
    plp_d = din("pl_proj", [DPLE, D])
    plg_d = din("pl_gate", [D, D])
    pc_d = din("pcols", [128, PC_N])
    cs_d = din("consts", [128, CS_N])
    out_d = nc.dram_tensor("out", [TOK, D], F32, kind="ExternalOutput").ap()
    h1s_d = nc.dram_tensor("h1s", [8, 128, TOK], F32, kind="Internal").ap()
    dbg_d = {}
    if dbg:
        for name, width in dbg.items():
            dbg_d[name] = nc.dram_tensor("dbg_" + name, [128, width], F32, kind="ExternalOutput").ap()

    glob = ExitStack()
    with glob:
        P = Prog(nc, glob)
        phase = [1]
        sems = {e: glob.enter_context(nc.semaphore("s_" + e)) for e in Prog.ENGS}
        h1s = [Buf("h1s%d" % kc, h1s_d) for kc in range(8)]
        cnt = [0]

        def mk_alloc(stack, init_dep_fn):
            def sb(shape, dt, name=None):
                cnt[0] += 1
                nm = (name or "t") + "_%d" % cnt[0]
                b = Buf(nm, stack.enter_context(nc.sbuf_tensor(nm, list(shape), dt)), init_dep_fn())
                P.allbufs.append(b)
                return b

            def ps(name=None):
                cnt[0] += 1
                nm = (name or "ps") + "_%d" % cnt[0]
                b = Buf(nm, stack.enter_context(nc.psum_tensor(nm, [128, 512], F32)), init_dep_fn())
                P.allbufs.append(b)
                return b
            return sb, ps

        def fsz(ap):
            n = 1
            for d_ in ap.shape[1:]:
                n *= d_
            return n

        def ecost(eng, ap, psum=False):
            n = fsz(ap)
            if eng == "act":
                return 190.0 + n / 1.15
            if eng == "dve":
                return (130.0 if psum else 70.0) + n / 0.96
            return 250.0 + n / 0.33

        def mm(o, oap, l, lap, r, rap, start=True, stop=True, tp=None):
            c_ = max(64, fsz(rap)) * (4 if lap.dtype == F32 else 1) / 1.95 + 8.0
            if tp is not None:
                c_ *= 0.6
            if tp is None:
                P.op("pe", lambda e: e.matmul(oap, lap, rap, start=start, stop=stop), reads=[l, r], writes=[o],
                     cost=c_, lat=180.0)
            else:
                P.op("pe", lambda e: e.matmul(oap, lap, rap, start=start, stop=stop, tile_position=tp),
                     reads=[l, r], writes=[o], cost=c_, lat=180.0)

        def tr(o, oap, i, iap):
            P.op("pe", lambda e: e.transpose(oap, iap, identF), reads=[i, cs], writes=[o], cost=140.0, lat=180.0)

        def act(o, oap, i, iap, func, bias=None, scale=None, extra=()):
            kw = {}
            if bias is not None:
                kw["bias"] = bias
            if scale is not None:
                kw["scale"] = scale
            P.op("act", lambda e: e.activation(out=oap, in_=iap, func=func, **kw),
                 reads=[i] + list(extra), writes=[o], cost=ecost("act", oap))

        def tt(eng, o, oap, a, aap, b, bap, op):
            P.op(eng, lambda e: e.tensor_tensor(out=oap, in0=aap, in1=bap, op=op), reads=[a, b], writes=[o],
                 cost=ecost(eng, oap, True))

        def ts(eng, o, oap, a, aap, s1, s2, op0, op1=None, extra=()):
            if op1 is None:
                P.op(eng, lambda e: e.tensor_scalar(out=oap, in0=aap, scalar1=s1, scalar2=None, op0=op0),
                     reads=[a] + list(extra), writes=[o], cost=ecost(eng, oap))
            else:
                P.op(eng, lambda e: e.tensor_scalar(out=oap, in0=aap, scalar1=s1, scalar2=s2, op0=op0, op1=op1),
                     reads=[a] + list(extra), writes=[o], cost=ecost(eng, oap))

        def stt(eng, o, oap, a, aap, s, b, bap, op0, op1, extra=()):
            P.op(eng, lambda e: e.scalar_tensor_tensor(out=oap, in0=aap, scalar=s, in1=bap, op0=op0, op1=op1),
                 reads=[a, b] + list(extra), writes=[o], cost=ecost(eng, oap, True))

        def cp(eng, o, oap, i, iap):
            if eng == "act":
                P.op("act", lambda e: e.copy(out=oap, in_=iap), reads=[i], writes=[o], cost=ecost("act", oap))
            else:
                P.op(eng, lambda e: e.tensor_copy(out=oap, in_=iap), reads=[i], writes=[o], cost=ecost(eng, oap, True))

        def dma(eng, o, oap, i, iap, sbuf_side):
            rd = [i] if i is not None else []
            wr = [o] if o is not None else []
            nbytes = fsz(oap) * oap.shape[0] * 4
            P.op(eng, lambda e: e.dma_start(out=oap, in_=iap), reads=rd, writes=wr, dma_buf=sbuf_side,
                 cost=(80.0 if eng == "sp" else 1200.0), lat=2000.0 + nbytes / 150.0)

        def rsqrt(o, oap, i, iap, eps=None, floor=None):
            if phase[0] == 1:
                if floor is None:
                    ec = epsc[:iap.shape[0], {0: 0, 1: 1}[eps]:{0: 1, 1: 2}[eps]]
                    P.op("act", lambda e: e.activation(out=oap, in_=iap, func=AF.Ln, bias=ec, scale=1.0),
                         reads=[i, epsc], writes=[o], cost=ecost("act", oap))
                else:
                    P.op("dve", lambda e: e.tensor_scalar(out=oap, in0=iap, scalar1=floor * floor, scalar2=None, op0=ALU.max),
                         reads=[i], writes=[o], cost=ecost("dve", oap, True))
                    P.op("act", lambda e: e.activation(out=oap, in_=oap, func=AF.Ln), reads=[o], writes=[o],
                         cost=ecost("act", oap))
                P.op("act", lambda e: e.activation(out=oap, in_=oap, func=AF.Exp, scale=-0.5), reads=[o], writes=[o],
                     cost=ecost("act", oap))
                return
            if floor is None:
                P.op("act", lambda e: e.activation(out=oap, in_=iap, func=AF.Sqrt, bias=epsc[:iap.shape[0], {0: 0, 1: 1}[eps]:{0: 1, 1: 2}[eps]], scale=1.0),
                     reads=[i, epsc], writes=[o], cost=ecost("act", oap))
            else:
                P.op("act", lambda e: e.activation(out=oap, in_=iap, func=AF.Sqrt), reads=[i], writes=[o], cost=ecost("act", oap))
                P.op("dve", lambda e: e.tensor_scalar(out=oap, in0=oap, scalar1=floor, scalar2=None, op0=ALU.max),
                     reads=[o], writes=[o], cost=ecost("dve", oap))
            P.op("dve", lambda e: e.reciprocal(out=oap, in_=oap), reads=[o], writes=[o], cost=ecost("dve", oap))

        def memset(eng, o, oap, val):
            P.op(eng, lambda e: e.memset(oap, val), writes=[o], cost=ecost(eng, oap))

        def dump(name, b, ap=None):
            if dbg and name in dbg_d:
                dma("sp", None, dbg_d[name], b, b[:] if ap is None else ap, b)

        sbG, psG = mk_alloc(glob, lambda: None)
        pc = sbG([128, PC_N], F32, "pc")
        cs = sbG([128, CS_N], F32, "cs")
        identb = sbG([128, 128], BF16, "identb")
        id64b = sbG([128, 64], BF16, "id64b")
        bones = sbG([128, 128], BF16, "bones")
        bavg = sbG([128, 128], BF16, "bavg")
        onesD = sbG([128, 128], BF16, "onesD")
        onesDf = sbG([128, 128], F32, "onesDf")
        fenceb = sbG([128, 2], F32, "fence")
        relb = sbG([128, 2], F32, "relb")
        omka = sbG([128, 4], F32, "omka")
        epsc = sbG([128, 2], F32, "epsc")
        PSB = [psG("pb") for _ in range(8)]
        psi = [0]

        PSSPLIT = {"x": [0, 1, 2], "e": [3, 4, 5], "h": [6, 7], "all": list(range(8))}
        psc = {"x": 0, "e": 0, "h": 0, "all": 0}
        pskind = ["x"]

        def nps():
            k_ = pskind[0]
            g_ = PSSPLIT[k_]
            b = PSB[g_[psc[k_] % len(g_)]]
            psc[k_] += 1
            return b

        def pn(pb):
            return pb[:, 0:N]

        def v2(ap):
            return ap.rearrange("p (b t) -> p b t", b=2)

        def vc(ap):
            return ap.rearrange("p (b c t) -> p b c t", b=2, c=NCHK)

        dma("sp", pc, pc[:], None, pc_d, pc)
        dma("sp", cs, cs[:], None, cs_d, cs)
        cp("dve", identb, identb[:], cs, cs[:, CS_ID:CS_ID + 128])
        cp("dve", id64b, id64b[:], cs, cs[:, CS_I64:CS_I64 + 64])
        memset("pool", bones, bones[:], 0.0)
        memset("pool", bones, bones[0:64, 0:64], 1.0)
        memset("pool", bones, bones[64:128, 64:128], 1.0)
        memset("pool", bavg, bavg[:], 0.0)
        memset("pool", bavg, bavg[0:64, 0:64], 1.0 / 64)
        memset("pool", bavg, bavg[64:128, 64:128], 1.0 / 64)
        memset("pool", onesD, onesD[:], 1.0 / 1024)
        memset("pool", onesDf, onesDf[:], 1.0 / 1024)
        memset("pool", epsc, epsc[:, 0:1], LN_EPS)
        memset("pool", epsc, epsc[:, 1:2], GN_EPS)
        ts("dve", omka, omka[:], pc, pc[:, PC_OFF["k_a"]:PC_OFF["k_a"] + 4], -1.0, 1.0, ALU.mult, ALU.add)

        def pcol(name, j):
            return pc[:, PC_OFF[name] + j:PC_OFF[name] + j + 1]

        identF = cs[:, CS_ID:CS_ID + 128]
        mask_sl = cs[:, CS_SL:CS_SL + 64]
        masks2 = cs[:, CS_SU:CS_SU + 128]

        def layer_norm_fm(src, gname, bname, tmp_b, tmp_f):
            xb, sq = tmp_b
            mean, rstd, nmr = tmp_f
            pm = nps()
            pv = nps()
            for kc in range(8):
                act(sq[kc % 2], sq[kc % 2][:], src[kc], src[kc][:], AF.Square)
                if xb is None:
                    mm(pm, pn(pm), onesDf, onesDf[:], src[kc], src[kc][:], start=(kc == 0), stop=(kc == 7))
                else:
                    cp("act", xb[kc % 2], xb[kc % 2][:], src[kc], src[kc][:])
                    mm(pm, pn(pm), onesD, onesD[:], xb[kc % 2], xb[kc % 2][:], start=(kc == 0), stop=(kc == 7))
                mm(pv, pn(pv), onesD, onesD[:], sq[kc % 2], sq[kc % 2][:], start=(kc == 0), stop=(kc == 7))
            cp("act", mean, mean[:], pm, pn(pm))
            tt("dve", nmr, nmr[:], mean, mean[:], mean, mean[:], ALU.mult)
            tt("dve", rstd, rstd[:], pv, pn(pv), nmr, nmr[:], ALU.subtract)
            rsqrt(rstd, rstd[:], rstd, rstd[:], eps=0)
            stt("dve", nmr, nmr[:], mean, mean[:], -1.0, rstd, rstd[:], ALU.mult, ALU.mult)
            for kc in range(8):
                tt("dve", src[kc], src[kc][:], src[kc], src[kc][:], rstd, rstd[:], ALU.mult)
                tt("dve", src[kc], src[kc][:], src[kc], src[kc][:], nmr, nmr[:], ALU.add)
                act(src[kc], src[kc][:], src[kc], src[kc][:], AF.Identity,
                    bias=pcol(bname, kc), scale=pcol(gname, kc), extra=[pc])

        st1 = ExitStack()
        with st1:
            sb, _ = mk_alloc(st1, lambda: None)
            w_in = sb([128, 8, IN_W], BF16, "w_in")
            w_out = sb([128, 8, D], BF16, "w_out")
            loraw = sb([128, RW], BF16, "loraw")
            glw = sb([128, RW], BF16, "glw")
            poolw = sb([128, 512], BF16, "poolw")
            for kc in range(8):
                dma("pool", w_in, w_in[:, kc, :], None, w_in_d[kc * 128:(kc + 1) * 128, :], w_in)
            dma("pool", loraw, loraw[:], None, lora_d, loraw)
            dma("pool", glw, glw[:], None, gl_d, glw)
            dma("pool", poolw, poolw[:], None, poolw_d, poolw)
            for kc in range(8):
                dma("pool", w_out, w_out[:, kc, :], None, w_out_d[kc * 128:(kc + 1) * 128, :], w_out)

            xt = [sb([128, D], F32, "xt") for _ in range(2)]
            bst = [sb([128, 2, 6], F32, "bst") for _ in range(2)]
            mv = [sb([128, 4], F32, "mv") for _ in range(2)]
            h0f_a = [[sb([128, N], F32, "h0f") for _ in range(8)] for _ in range(2)]
            h0b_a = [[sb([128, N], BF16, "h0b") for _ in range(8)] for _ in range(2)]
            zst = [sb([128, 2, TB + 1], F32, "zst") for _ in range(3)]
            zc = sb([128, 14, 2], F32, "zc")
            zs = [sb([128, N], F32, "zs") for _ in range(12)]
            ztmp = [sb([128, N], F32, "ztmp") for _ in range(2)]
            lin = sb([128, N], BF16, "lin")
            gdin = sb([128, N], BF16, "gdin")
            u = [sb([128, 2, HALO + TB], F32, "u") for _ in range(4)]
            s2 = [sb([128, 2, HALO + TB], F32, "s2") for _ in range(2)]
            dsig = sb([128, 4, N], F32, "dsig")
            av = sb([128, 4, N], F32, "av")
            gv = sb([128, 4, N], F32, "gv")
            eL = sb([128, 4, N], F32, "eL")
            scm = sb([128, N], F32, "scm")
            tA = [sb([128, N], F32, "tA") for _ in range(8)]
            tB16 = [sb([128, N], BF16, "tB16") for _ in range(3)]
            AR = [sb([128, 2, NCHK, 2, CH], BF16, "AR") for _ in range(4)]
            BT = [sb([128, N], BF16, "BT") for _ in range(4)]
            KT = [sb([128, N], BF16, "KT") for _ in range(4)]
            VT = [sb([128, N], BF16, "VT") for _ in range(4)]
            bon = [sb([128, N], F32, "bon") for _ in range(4)]
            yT = sb([128, 4, 2, NCHK, CH], F32, "yT")
            ycat = [sb([128, N], BF16, "ycat") for _ in range(8)]
            Sf = sb([128, 8, CH], F32, "Sf")
            Sb = sb([128, 8, CH], BF16, "Sb")
            NA = [sb([128, 8, 2, CH], BF16, "NA") for _ in range(2)]
            KA = [sb([128, 8, 2, CH], BF16, "KA") for _ in range(2)]
            tokK = [sb([128, 8, CH], BF16, "tokK") for _ in range(2)]
            tokB = [sb([128, 8, CH], BF16, "tokB") for _ in range(2)]
            tokV = [sb([128, 8, CH], BF16, "tokV") for _ in range(2)]
            Pm = [sb([128, 8, CH], BF16, "Pm") for _ in range(3)]
            PTm = [sb([128, 8, CH], BF16, "PTm") for _ in range(3)]
            Xm = [sb([128, 8, CH], BF16, "Xm") for _ in range(3)]
            RHSb = sb([128, 8, CH], BF16, "RHSb")
            Ub = sb([128, 8, CH], BF16, "Ub")
            dSt = sb([128, 8, CH], F32, "dSt")
            pre = [sb([128, N], F32, "pre") for _ in range(8)]
            lnb = (None, [sb([128, N], BF16, "lnsq") for _ in range(2)])
            lnf = (sb([128, N], F32, "mean"), sb([128, N], F32, "rstd"), sb([128, N], F32, "nmr"))

            memset("pool", zc, zc[:], 0.0)
            for g in range(4):
                memset("pool", u[g], u[g][:], 0.0)
            memset("pool", Sf, Sf[:], 0.0)
            memset("pool", Sb, Sb[:], 0.0)
            memset("pool", scm, scm[:], 1.0)
            memset("pool", scm, scm[:].rearrange("p (c t) -> p c t", t=CH)[:, :, 0:1], 0.0)

            def tokslice(b, c):
                return slice(b * TB + c * CH, b * TB + (c + 1) * CH)

            for blk in range(nblk1):
                last = (blk == nblk1 - 1)
                h0f, h0b = h0f_a[blk % 2], h0b_a[blk % 2]
                pskind[0] = "x"
                xs = []
                for t2 in range(2):
                    i_ = t2
                    r0 = t2 * SEQ + blk * TB
                    X, S_, M_ = xt[i_], bst[i_], mv[i_]
                    xs.append(X)
                    dma("sp", X, X[:], None, x_d[r0:r0 + 128, :], X)
                    for c2 in range(2):
                        P.op("dve", lambda e, X=X, S_=S_, c2=c2: e.bn_stats(out=S_.t[:, c2, :], in_=X.t[:, c2 * 512:(c2 + 1) * 512]),
                             reads=[X], writes=[S_])
                    P.op("dve", lambda e, S_=S_, M_=M_: e.bn_aggr(out=M_.t[:, 0:2], in_=S_.t[:]),
                         reads=[S_], writes=[M_])
                    rsqrt(M_, M_[:, 2:3], M_, M_[:, 1:2], eps=0)
                    ts("dve", X, X[:], X, X[:], M_[:, 0:1], M_[:, 2:3], ALU.subtract, ALU.mult, extra=[M_])
                for kc in range(8):
                    pb = nps()
                    for t2 in range(2):
                        tr(pb, pb[:, t2 * 128:(t2 + 1) * 128], xs[t2], xs[t2][:, kc * 128:(kc + 1) * 128])
                    act(h0f[kc], h0f[kc][:], pb, pn(pb), AF.Identity, bias=pcol("ln_in_b", kc),
                        scale=pcol("ln_in_g", kc), extra=[pc])
                    act(h0b[kc], h0b[kc][:], pb, pn(pb), AF.Identity, bias=pcol("ln_in_b", kc),
                        scale=pcol("ln_in_g", kc), extra=[pc])
                if last:
                    dump("h0f0", h0f[0])

                for n in range(18):
                    pb = nps()
                    for kc in range(8):
                        mm(pb, pn(pb), w_in, w_in[:, kc, n * 128:(n + 1) * 128], h0b[kc], h0b[kc][:],
                           start=(kc == 0), stop=(kc == 7))
                    pbv = v2(pn(pb))
                    if n >= 14:
                        g = n - 14
                        cp("act", u[g], u[g][:, :, HALO:HALO + TB], pb, pbv)
                        continue
                    Z = zst[n % 3]
                    cp("act", Z, Z[:, :, 1:TB + 1], pb, pbv)
                    cp("dve", Z, Z[:, :, 0:1], zc, zc[:, n, :].unsqueeze(2))
                    cp("dve", zc, zc[:, n, :].unsqueeze(2), Z, Z[:, :, TB:TB + 1])
                    T_ = ztmp[n % 2]
                    Tv = v2(T_[:])
                    tt("dve", T_, Tv, Z, Z[:, :, 0:TB], Z, Z[:, :, 1:TB + 1], ALU.subtract)
                    if n < 12:
                        dst = zs[n]
                        stt("dve", dst, v2(dst[:]), T_, Tv, pcol("mu", n),
                            Z, Z[:, :, 1:TB + 1], ALU.mult, ALU.add, extra=[pc])
                    else:
                        stt("dve", T_, Tv, T_, Tv, pcol("mu", n), Z, Z[:, :, 1:TB + 1], ALU.mult, ALU.add, extra=[pc])
                        if n == 12:
                            act(lin, lin[0:64, :], T_, T_[0:64, :], AF.Tanh)
                            cp("act", lin, lin[64:128, :], T_, T_[64:128, :])
                        else:
                            act(gdin, gdin[:], T_, T_[:], AF.Sigmoid)
                if last:
                    dump("zs0", zs[0])
                    dump("zs4", zs[4])

                for jc in range(4):
                    js = slice(jc * 128, (jc + 1) * 128)
                    pb = nps()
                    mm(pb, pn(pb), loraw, loraw[0:64, js], lin, lin[0:64, :])
                    act(dsig, dsig[:, jc, :], pb, pn(pb), AF.Sigmoid, bias=pcol("w0", jc), extra=[pc])
                    pb = nps()
                    mm(pb, pn(pb), loraw, loraw[64:128, js], lin, lin[64:128, :])
                    act(av, av[:, jc, :], pb, pn(pb), AF.Sigmoid, bias=pcol("a0", jc), extra=[pc])
                    pb = nps()
                    mm(pb, pn(pb), glw, glw[:, js], gdin, gdin[:])
                    cp("act", gv, gv[:, jc, :], pb, pn(pb))
                if last:
                    dump("dsig", dsig, dsig[:].rearrange("p j n -> p (j n)"))
                    dump("av", av, av[:].rearrange("p j n -> p (j n)"))
                    dump("gv", gv, gv[:].rearrange("p j n -> p (j n)"))

                for jc in range(4):
                    r_, k_, v_ = zs[jc], zs[4 + jc], zs[8 + jc]
                    kk, rinv, k2, bv, Ls, Lx, enL, eLx = tA
                    act(kk, kk[:], k_, k_[:], AF.Copy, scale=pcol("k_k", jc), extra=[pc])
                    act(tB16[0], tB16[0][:], kk, kk[:], AF.Square)
                    pb = nps()
                    mm(pb, pn(pb), bones, bones[:], tB16[0], tB16[0][:])
                    rsqrt(rinv, rinv[:], pb, pn(pb), floor=1e-12)
                    tt("dve", kk, kk[:], kk, kk[:], rinv, rinv[:], ALU.mult)
                    act(k2, k2[:], av, av[:, jc, :], AF.Identity, bias=omka[:, jc:jc + 1], scale=pcol("k_a", jc),
                        extra=[pc, omka])
                    tt("dve", k2, k2[:], k2, k2[:], k_, k_[:], ALU.mult)
                    tt("dve", bv, bv[:], kk, kk[:], av, av[:, jc, :], ALU.mult)
                    P.op("dve", lambda e, Ls=Ls, jc=jc: e.tensor_tensor_scan(
                        out=Ls.t[:], data0=scm.t[:], data1=dsig.t[:, jc, :], initial=0.0, op0=ALU.mult, op1=ALU.add),
                        reads=[scm, dsig], writes=[Ls])
                    tt("dve", Lx, Lx[:], Ls, Ls[:], dsig, dsig[:, jc, :], ALU.subtract)
                    act(eL, eL[:, jc, :], Ls, Ls[:], AF.Exp, scale=-CDEC)
                    act(enL, enL[:], Ls, Ls[:], AF.Exp, scale=CDEC)
                    act(eLx, eLx[:], Lx, Lx[:], AF.Exp, scale=-CDEC)
                    ARv = AR[jc]
                    tt("dve", ARv, ARv[:, :, :, 1, :], r_, vc(r_[:]), eL, vc(eL[:, jc, :]), ALU.mult)
                    stt("dve", ARv, ARv[:, :, :, 0, :], kk, vc(kk[:]), -1.0, eLx, vc(eLx[:]), ALU.mult, ALU.mult)
                    tt("dve", BT[jc], BT[jc][:], bv, bv[:], enL, enL[:], ALU.mult)
                    tt("dve", KT[jc], KT[jc][:], k2, k2[:], enL, enL[:], ALU.mult)
                    cp("act", VT[jc], VT[jc][:], v_, v_[:])
                    stt("dve", tB16[1], tB16[1][:], r_, r_[:], pcol("r_k", jc), k2, k2[:], ALU.mult, ALU.mult, extra=[pc])
                    pb = nps()
                    mm(pb, pn(pb), bones, bones[:], tB16[1], tB16[1][:])
                    tt("dve", bon[jc], bon[jc][:], pb, pn(pb), v_, v_[:], ALU.mult)
                    if last and jc == 0:
                        dump("kkn", kk)
                        dump("k2", k2)
                        dump("Ls", Ls)
                if last:
                    dump("eL", eL, eL[:].rearrange("p j n -> p (j n)"))
                    dump("bon0", bon[0])

                pskind[0] = "e"
                for c in range(NCHK):
                    par2 = c % 2
                    NAc, KAc, tK, tBk, tV = NA[par2], KA[par2], tokK[par2], tokB[par2], tokV[par2]
                    for (lhs, dstb) in ((BT, NAc), (KT, KAc)):
                        for grp in range(2):
                            pb = nps()
                            for q4 in range(4):
                                q = grp * 4 + q4
                                jc, b = q // 2, q % 2
                                for par in range(2):
                                    ps_ = slice(par * 64, par * 64 + 64)
                                    mm(pb, pb[ps_, q4 * 128:(q4 + 1) * 128], lhs[jc], lhs[jc][ps_, tokslice(b, c)],
                                       AR[jc], AR[jc][ps_, b, c, :, :].rearrange("p a t -> p (a t)"), tp=(par * 64, par * 64))
                            tt("dve", dstb, dstb[:, grp * 4:(grp + 1) * 4, :, :].rearrange("p q a t -> p q (a t)"),
                               pb, pb[:].rearrange("p (q x) -> p q x", q=4),
                               cs, masks2.unsqueeze(1).to_broadcast([128, 4, 128]), ALU.mult)
                    pb = nps()
                    for q in range(8):
                        jc, b = q // 2, q % 2
                        for par in range(2):
                            ps_ = slice(par * 64, par * 64 + 64)
                            mm(pb, pb[ps_, q * 64:(q + 1) * 64], AR[jc], AR[jc][ps_, b, c, 0, :],
                               BT[jc], BT[jc][ps_, tokslice(b, c)], tp=(par * 64, par * 64))
                    PT0 = PTm[0]
                    tt("dve", PT0, PT0[:], pb, pb[:].rearrange("p (q t) -> p q t", q=8),
                       cs, mask_sl.unsqueeze(1).to_broadcast([128, 8, 64]), ALU.mult)
                    for (src, dstb, eng) in ((KT, tK, "act"), (BT, tBk, "dve"), (VT, tV, "act")):
                        pb = nps()
                        for q in range(8):
                            jc, b = q // 2, q % 2
                            for par in range(2):
                                ps_ = slice(par * 64, par * 64 + 64)
                                mm(pb, pb[ps_, q * 64:(q + 1) * 64], src[jc], src[jc][ps_, tokslice(b, c)],
                                   identb, identb[ps_, ps_], tp=(par * 64, par * 64))
                        cp(eng, dstb, dstb[:], pb, pb[:].rearrange("p (q t) -> p q t", q=8))
                    X0 = Xm[0]
                    tt("dve", X0, X0[:], NAc, NAc[:, :, 0, :], id64b, id64b[:].unsqueeze(1).to_broadcast([128, 8, 64]), ALU.add)
                    Pc, PTc, Xc = None, PT0, X0
                    def Pap(ps_, q):
                        return (NAc, NAc[ps_, q, 0, :]) if Pc is None else (Pc, Pc[ps_, q, :])

                    for lvl in range(5):
                        PTn = PTm[(lvl + 1) % 3]
                        pb = nps()
                        for q in range(8):
                            for par in range(2):
                                ps_ = slice(par * 64, par * 64 + 64)
                                pb_, pa_ = Pap(ps_, q)
                                mm(pb, pb[ps_, q * 64:(q + 1) * 64], pb_, pa_, PTc, PTc[ps_, q, :],
                                   tp=(par * 64, par * 64))
                        cp("act", PTn, PTn[:], pb, pb[:].rearrange("p (q t) -> p q t", q=8))
                        if lvl < 4:
                            Pn = Pm[(lvl + 1) % 3]
                            pb = nps()
                            for q in range(8):
                                for par in range(2):
                                    ps_ = slice(par * 64, par * 64 + 64)
                                    pb_, pa_ = Pap(ps_, q)
                                    mm(pb, pb[ps_, q * 64:(q + 1) * 64], PTc, PTc[ps_, q, :], pb_, pa_,
                                       tp=(par * 64, par * 64))
                            cp("act", Pn, Pn[:], pb, pb[:].rearrange("p (q t) -> p q t", q=8))
                        else:
                            Pn = None
                        Xn = Xm[(lvl + 1) % 3]
                        pb = nps()
                        for q in range(8):
                            for par in range(2):
                                ps_ = slice(par * 64, par * 64 + 64)
                                mm(pb, pb[ps_, q * 64:(q + 1) * 64], PTn, PTn[ps_, q, :], Xc, Xc[ps_, q, :],
                                   tp=(par * 64, par * 64))
                        tt("dve", Xn, Xn[:], pb, pb[:].rearrange("p (q t) -> p q t", q=8), Xc, Xc[:], ALU.add)
                        Pc, PTc, Xc = Pn, PTn, Xn
                    pb = nps()
                    for q in range(8):
                        jc, b = q // 2, q % 2
                        for par in range(2):
                            ps_ = slice(par * 64, par * 64 + 64)
                            tp = (par * 64, par * 64)
                            mm(pb, pb[ps_, q * 64:(q + 1) * 64], AR[jc], AR[jc][ps_, b, c, 0, :], Sb, Sb[ps_, q, :],
                               start=True, stop=False, tp=tp)
                            mm(pb, pb[ps_, q * 64:(q + 1) * 64], KAc, KAc[ps_, q, 0, :], tV, tV[ps_, q, :],
                               start=False, stop=True, tp=tp)
                    cp("act", RHSb, RHSb[:], pb, pb[:].rearrange("p (q t) -> p q t", q=8))
                    pb = nps()
                    for q in range(8):
                        for par in range(2):
                            ps_ = slice(par * 64, par * 64 + 64)
                            mm(pb, pb[ps_, q * 64:(q + 1) * 64], Xc, Xc[ps_, q, :], RHSb, RHSb[ps_, q, :],
                               tp=(par * 64, par * 64))
                    cp("dve", Ub, Ub[:], pb, pb[:].rearrange("p (q t) -> p q t", q=8))
                    pby = nps()
                    for q in range(8):
                        jc, b = q // 2, q % 2
                        for par in range(2):
                            ps_ = slice(par * 64, par * 64 + 64)
                            tp = (par * 64, par * 64)
                            o_ = pby[ps_, q * 64:(q + 1) * 64]
                            mm(pby, o_, Sb, Sb[ps_, q, :], AR[jc], AR[jc][ps_, b, c, 1, :], start=True, stop=False, tp=tp)
                            mm(pby, o_, Ub, Ub[ps_, q, :], NAc, NAc[ps_, q, 1, :], start=False, stop=False, tp=tp)
                            mm(pby, o_, tV, tV[ps_, q, :], KAc, KAc[ps_, q, 1, :], start=False, stop=True, tp=tp)
                    pbs = nps()
                    for q in range(8):
                        for par in range(2):
                            ps_ = slice(par * 64, par * 64 + 64)
                            tp = (par * 64, par * 64)
                            o_ = pbs[ps_, q * 64:(q + 1) * 64]
                            mm(pbs, o_, tK, tK[ps_, q, :], tV, tV[ps_, q, :], start=True, stop=False, tp=tp)
                            mm(pbs, o_, tBk, tBk[ps_, q, :], Ub, Ub[ps_, q, :], start=False, stop=True, tp=tp)
                    tt("dve", dSt, dSt[:], pbs, pbs[:].rearrange("p (q t) -> p q t", q=8), Sf, Sf[:], ALU.add)
                    tt("dve", Sf, Sf[:].rearrange("p (j b) v -> p j b v", b=2),
                       dSt, dSt[:].rearrange("p (j b) v -> p j b v", b=2),
                       eL, eL[:].rearrange("p j (b c t) -> p j b c t", b=2, c=NCHK)[:, :, :, c, CH - 1:CH].to_broadcast([128, 4, 2, CH]),
                       ALU.mult)
                    cp("act", Sb, Sb[:], Sf, Sf[:])
                    cp("act", yT, yT[:, :, :, c, :], pby, pby[:].rearrange("p (j b t) -> p j b t", j=4, b=2))
                if last:
                    dump("yT", yT, yT[:].rearrange("p j b c t -> p (j b c t)"))
                    dump("Sf", Sf, Sf[:].rearrange("p q v -> p (q v)"))

                pskind[0] = "h"
                for jc in range(4):
                    y_ = yT[:, jc].rearrange("p b c t -> p (b c t)")
                    d_, yn = tA[0], tA[1]
                    cp("act", tB16[0], tB16[0][:], yT, y_)
                    pb = nps()
                    mm(pb, pn(pb), bavg, bavg[:], tB16[0], tB16[0][:])
                    tt("dve", d_, d_[:], yT, y_, pb, pn(pb), ALU.subtract)
                    act(tB16[1], tB16[1][:], d_, d_[:], AF.Square)
                    pb = nps()
                    mm(pb, pn(pb), bavg, bavg[:], tB16[1], tB16[1][:])
                    rsqrt(yn, yn[:], pb, pn(pb), eps=1)
                    tt("dve", yn, yn[:], yn, yn[:], d_, d_[:], ALU.mult)
                    act(yn, yn[:], yn, yn[:], AF.Identity, bias=pcol("lnx_b", jc), scale=pcol("lnx_g", jc), extra=[pc])
                    tt("dve", yn, yn[:], yn, yn[:], bon[jc], bon[jc][:], ALU.add)
                    tt("dve", ycat[jc], ycat[jc][:], yn, yn[:], gv, gv[:, jc, :], ALU.mult)

                for g in range(4):
                    W = 2 << g
                    U_ = u[g]
                    sa, sbb = s2[0], s2[1]
                    L_ = HALO + TB
                    tt("pool", sa, sa[:, :, 1:L_], U_, U_[:, :, 1:L_], U_, U_[:, :, 0:L_ - 1], ALU.add)
                    cur, oth = sa, sbb
                    sh = 2
                    lo = 1
                    while sh < W:
                        lo2 = lo + sh
                        tt("pool", oth, oth[:, :, lo2:L_], cur, cur[:, :, lo2:L_], cur, cur[:, :, lo2 - sh:L_ - sh], ALU.add)
                        cur, oth = oth, cur
                        lo = lo2
                        sh *= 2
                    dT = tB16[2]
                    dTv = v2(dT[:])
                    stt("dve", dT, dTv, cur, cur[:, :, HALO:L_], 1.0 / W, U_, U_[:, :, HALO:L_], ALU.mult, ALU.subtract)
                    if blk == 0:
                        rc = cs[:, CS_RC + g * 16:CS_RC + (g + 1) * 16].unsqueeze(1).to_broadcast([128, 2, 16])
                        tmpv = v2(tA[2][:])[:, :, 0:16]
                        tt("dve", tA[2], tmpv, cur, cur[:, :, HALO:HALO + 16], cs, rc, ALU.mult)
                        tt("dve", dT, dTv[:, :, 0:16], tA[2], tmpv, U_, U_[:, :, HALO:HALO + 16], ALU.subtract)
                    pb = nps()
                    mm(pb, pn(pb), poolw, poolw[:, g * 128:(g + 1) * 128], dT, dT[:])
                    act(ycat[4 + g], ycat[4 + g][:], pb, pn(pb), AF.Identity, scale=pcol("pool_scale", g), extra=[pc])
                    cp("pool", U_, U_[:, :, 0:HALO], U_, U_[:, :, TB:TB + HALO])

                for oc in range(8):
                    pb = nps()
                    for kc in range(8):
                        mm(pb, pn(pb), w_out, w_out[:, kc, oc * 128:(oc + 1) * 128], ycat[kc], ycat[kc][:],
                           start=(kc == 0), stop=(kc == 7))
                    stt("dve", pre[oc], pre[oc][:], h0f[oc], h0f[oc][:], ALPHA, pb, pn(pb), ALU.mult, ALU.add)
                if last:
                    dump("pre0", pre[0])
                layer_norm_fm(pre, "ln1_g", "ln1_b", lnb, lnf)
                if last:
                    dump("h1_0", pre[0])
                for kc in range(8):
                    dma("sp", h1s[kc], h1s_d[kc].rearrange("p (b s) -> p b s", b=2)[:, :, blk * TB:(blk + 1) * TB],
                        pre[kc], v2(pre[kc][:]), pre[kc])
            w_in_rel = P.op("pool", lambda e: e.memset(relb.t[:], 0.0), reads=[w_in], writes=[w_in, relb],
                            cost=300.0)
            w_in_addr = (nc.lookup_mloc(w_in.t).addr, int(nc.lookup_mloc(w_in.t).dims[1]))
            P.fence(fenceb)

        phase[0] = 2
        st2 = ExitStack()
        with st2:
            fo = P.fence_op
            sb, _ = mk_alloc(st2, lambda: fo)
            NKA = 6
            sbE, _ = mk_alloc(st2, lambda: w_in_rel)
            wgA = sbE([128, NKA, DFF], BF16, "wgA")
            a0_, n0_ = nc.lookup_mloc(wgA.t).addr, int(nc.lookup_mloc(wgA.t).dims[1])
            assert a0_ >= w_in_addr[0] and a0_ + n0_ <= w_in_addr[0] + w_in_addr[1], (a0_, n0_, w_in_addr)
            w_in_end = w_in_addr[0] + w_in_addr[1]
            ptE = [sbE([128, DPLE], F32, "pt") for _ in range(3)]
            mlE = nc.lookup_mloc(ptE[-1].t)
            assert mlE.addr + int(mlE.dims[1]) == w_in_end, (mlE.addr, int(mlE.dims[1]), w_in_end)
            wgB = sb([128, 8 - NKA, DFF], BF16, "wgB")
            assert nc.lookup_mloc(wgB.t).addr >= w_in_end

            def wgsel(kc, fs):
                if kc < NKA:
                    return wgA, wgA[:, kc, fs]
                return wgB, wgB[:, kc - NKA, fs]
            wu = sb([128, 8, DFF], BF16, "wu")
            wd = sb([128, NFC, D], BF16, "wd")
            plg = sb([128, 8, D], BF16, "plg")
            plp = sb([128, 2, D], BF16, "plp")
            if nblk2 > 0:
                for kc in range(NKA):
                    dma("pool", wgA, wgA[:, kc, :], None, wg_d[kc * 128:(kc + 1) * 128, :], wgA)
                for kc in range(8):
                    if kc >= NKA:
                        dma("pool", wgB, wgB[:, kc - NKA, :], None, wg_d[kc * 128:(kc + 1) * 128, :], wgB)
                    dma("pool", wu, wu[:, kc, :], None, wu_d[kc * 128:(kc + 1) * 128, :], wu)
                for kc in range(8):
                    dma("pool", plg, plg[:, kc, :], None, plg_d[kc * 128:(kc + 1) * 128, :], plg)
                for kc in range(2):
                    dma("pool", plp, plp[:, kc, :], None, plp_d[kc * 128:(kc + 1) * 128, :], plp)
                for fc in range(NFC):
                    dma("pool", wd, wd[:, fc, :], None, wd_d[fc * 128:(fc + 1) * 128, :], wd)
            h1f = [sb([128, N], F32, "h1f") for _ in range(8)]
            h1b = [sb([128, N], BF16, "h1b") for _ in range(8)]
            pt = ptE + [sb([128, DPLE], F32, "pt")]
            pTb = [sb([128, N], BF16, "pTb") for _ in range(2)]
            sg = [sb([128, N], F32, "sg") for _ in range(2)]
            actb = [sb([128, N], BF16, "actb") for _ in range(NFC)]
            et = [sb([128, N], F32, "et") for _ in range(2)]
            pre2 = [sb([128, N], F32, "pre2") for _ in range(8)]
            lnb2 = ([sb([128, N], BF16, "lnxb") for _ in range(2)], [sb([128, N], BF16, "lnsq") for _ in range(2)])
            lnf2 = (sb([128, N], F32, "mean"), sb([128, N], F32, "rstd"), sb([128, N], F32, "nmr"))
            ot = [sb([128, D], F32, "ot") for _ in range(1)]

            pskind[0] = "all"
            for blk in range(nblk2):
                for kc in range(8):
                    dma("sp", h1f[kc], v2(h1f[kc][:]), h1s[kc],
                        h1s_d[kc].rearrange("p (b s) -> p b s", b=2)[:, :, blk * TB:(blk + 1) * TB], h1f[kc])
                    cp("act", h1b[kc], h1b[kc][:], h1f[kc], h1f[kc][:])
                pts = []
                for t2 in range(2):
                    i_ = (blk % 2) * 2 + t2
                    r0 = t2 * SEQ + blk * TB
                    dma("sp", pt[i_], pt[i_][:], None, p_d[r0:r0 + 128, :], pt[i_])
                    pts.append(pt[i_])
                for k2_ in range(2):
                    pb = nps()
                    for t2 in range(2):
                        tr(pb, pb[:, t2 * 128:(t2 + 1) * 128], pts[t2], pts[t2][:, k2_ * 128:(k2_ + 1) * 128])
                    cp("act", pTb[k2_], pTb[k2_][:], pb, pn(pb))
                for fc in range(NFC):
                    fs = slice(fc * 128, (fc + 1) * 128)
                    pg = nps()
                    for kc in range(8):
                        wb_, wa_ = wgsel(kc, fs)
                        mm(pg, pn(pg), wb_, wa_, h1b[kc], h1b[kc][:], start=(kc == 0), stop=(kc == 7))
                    pu = nps()
                    for kc in range(8):
                        mm(pu, pn(pu), wu, wu[:, kc, fs], h1b[kc], h1b[kc][:], start=(kc == 0), stop=(kc == 7))
                    S_ = sg[fc % 2]
                    act(S_, S_[:], pg, pn(pg), AF.Silu)
                    tt("dve", actb[fc], actb[fc][:], S_, S_[:], pu, pn(pu), ALU.mult)
                for oc in range(8):
                    os_ = slice(oc * 128, (oc + 1) * 128)
                    pe_ = nps()
                    for kc in range(8):
                        mm(pe_, pn(pe_), plg, plg[:, kc, os_], h1b[kc], h1b[kc][:], start=(kc == 0), stop=(kc == 7))
                    pp = nps()
                    for k2_ in range(2):
                        mm(pp, pn(pp), plp, plp[:, k2_, os_], pTb[k2_], pTb[k2_][:], start=(k2_ == 0), stop=(k2_ == 1))
                    E_ = et[oc % 2]
                    act(E_, E_[:], pe_, pn(pe_), AF.Sigmoid)
                    tt("dve", E_, E_[:], E_, E_[:], pp, pn(pp), ALU.mult)
                    pf = nps()
                    for fc in range(NFC):
                        mm(pf, pn(pf), wd, wd[:, fc, os_], actb[fc], actb[fc][:], start=(fc == 0), stop=(fc == NFC - 1))
                    stt("dve", pre2[oc], pre2[oc][:], h1f[oc], h1f[oc][:], ALPHA, pf, pn(pf), ALU.mult, ALU.add)
                    tt("dve", pre2[oc], pre2[oc][:], pre2[oc], pre2[oc][:], E_, E_[:], ALU.add)
                layer_norm_fm(pre2, "ln2_g", "ln2_b", lnb2, lnf2)
                for t2 in range(2):
                    r0 = t2 * SEQ + blk * TB
                    O_ = ot[0]
                    for hh in range(2):
                        pb = nps()
                        for kq in range(4):
                            kc = hh * 4 + kq
                            tr(pb, pb[:, kq * 128:(kq + 1) * 128], pre2[kc], pre2[kc][:, t2 * 128:(t2 + 1) * 128])
                        cp("act" if hh == 0 else "dve", O_, O_[:, hh * 512:(hh + 1) * 512], pb, pb[:])
                    dma("sp", None, out_d[r0:r0 + 128, :], O_, O_[:], O_)

            with nc.Block() as block:
                P.emit(sems, block)
            _NC_CACHE["est_ns"] = getattr(P, "est_ns", None)
    return nc


def _consts():
    cs = np.zeros((128, CS_N), np.float32)
    cs[:, CS_ID:CS_ID + 128] = np.eye(128, dtype=np.float32)
    pi = np.arange(128)[:, None] % 64
    ci = np.arange(64)[None, :]
    cs[:, CS_SU:CS_SU + 64] = (pi < ci)
    cs[:, CS_IU:CS_IU + 64] = (pi <= ci)
    cs[:, CS_SL:CS_SL + 64] = (pi > ci)
    cs[:, CS_I64:CS_I64 + 64] = (pi == ci)
    for g, W in enumerate((2, 4, 8, 16)):
        cnt = np.minimum(np.arange(1, 17), W).astype(np.float32)
        cs[:, CS_RC + g * 16:CS_RC + (g + 1) * 16] = (1.0 / cnt)[None, :]
    return cs


def _pcols(inp):
    pc = np.zeros((128, PC_N), np.float32)
    for name, ncol in PC_LAYOUT:
        v = np.asarray(inp["mu_shift" if name == "mu" else name], np.float32).reshape(-1)
        pc[:, PC_OFF[name]:PC_OFF[name] + ncol] = v.reshape(ncol, 128).T
    return pc


def _shared_inputs(inp):
    f = lambda a: np.ascontiguousarray(np.asarray(a, np.float32))
    sh = {
        "w_in": f(inp["w_in"][0]),
        "w_out": f(inp["w_out"][0]),
        "lora_w": f(np.concatenate([inp["wl_up"][0], inp["al_up"][0]], axis=0)),
        "gl_up": f(inp["gl_up"][0]),
        "pool_w": f(np.transpose(np.asarray(inp["pool_w"][0]), (1, 0, 2)).reshape(128, 512)),
        "ffn_gate": f(inp["ffn_gate"][0]),
        "ffn_up": f(inp["ffn_up"][0]),
        "ffn_down": f(inp["ffn_down"][0]),
        "pl_proj": f(inp["pl_proj"][0]),
        "pl_gate": f(inp["pl_gate"][0]),
        "pcols": _pcols(inp),
        "consts": _consts(),
    }
    return sh


def kernel(**inp):
    x = np.asarray(inp["x"], np.float32)
    p = np.asarray(inp["p"], np.float32)[0]
    if "nc" not in _NC_CACHE:
        _NC_CACHE["nc"] = build_program()
    nc = _NC_CACHE["nc"]
    sh = _shared_inputs(inp)
    in_maps = []
    for c in range(NCORE):
        m = dict(sh)
        m["x"] = np.ascontiguousarray(x[c * BPC:(c + 1) * BPC].reshape(TOK, D))
        m["p"] = np.ascontiguousarray(p[c * BPC:(c + 1) * BPC].reshape(TOK, DPLE))
        in_maps.append(m)
    res = run_bass_kernel_spmd(nc, in_maps, core_ids=list(range(NCORE)))
    out = np.stack([res.results[c]["out"].reshape(BPC, SEQ, D) for c in range(NCORE)], axis=0)
    return out.reshape(NCORE * BPC, SEQ, D).astype(np.float32)
```
